# Optimizing a Trainium2 kernel written in Bass

```python
import jax
import jax.numpy as jnp
from jax import lax
import numpy as np

D_MODEL = 1024
BATCH = 8
SEQ = 4096
DEPTH = 2

HEAD_DIM = 64
NSA_HEADS = 6
NSA_KV_GROUPS = 2
NSA_HPG = NSA_HEADS // NSA_KV_GROUPS
SB_HEADS = 5
MLA_HEADS = 5
NSA_D = NSA_HEADS * HEAD_DIM
NSA_KV = NSA_KV_GROUPS * HEAD_DIM
SB_D = SB_HEADS * HEAD_DIM
MLA_NOPE = 64
MLA_ROPE = 32
MLA_V = 64
MLA_Q_RANK = 192
MLA_KV_RANK = 128
MLA_D = MLA_HEADS * MLA_V
D_MIX = NSA_D + SB_D + MLA_D
CMP_LEN = 32
CMP_STRIDE = 16
CMP_HIDDEN = 128
SEL_BLOCK = 64
SEL_TOPK = 16
WINDOW = 512
Q_BLOCK = 128
ROPE_THETA = 10000.0
EPS = 1e-6
NEG_INF = -1e30
SEL_FORCE = 1e9

IN_WIDTHS = (
    NSA_D,
    NSA_KV, NSA_KV, NSA_KV, NSA_KV, NSA_KV, NSA_KV,
    NSA_HEADS * 3,
    NSA_D,
    SB_D, SB_D, SB_D, SB_D,
    MLA_Q_RANK, MLA_KV_RANK, MLA_ROPE,
    MLA_D,
)
D_IN = sum(IN_WIDTHS)
IN_SPLITS = tuple(int(v) for v in np.cumsum(IN_WIDTHS)[:-1])

kernel_name = 'hybrid_nsa_stickbreak_mla_block'


def rmsnorm(x, g):
    xf = x.astype(jnp.float32)
    y = xf * lax.rsqrt(jnp.mean(xf * xf, axis=-1, keepdims=True) + EPS)
    return (y * g.astype(jnp.float32)).astype(x.dtype)


def rope(x, pos):
    half = x.shape[-1] // 2
    inv = ROPE_THETA ** (-jnp.arange(half, dtype=jnp.float32) / half)
    ang = pos.astype(jnp.float32)[..., None] * inv
    ang = ang.reshape((ang.shape[0],) + (1,) * (x.ndim - 3) + ang.shape[1:])
    cos, sin = jnp.cos(ang), jnp.sin(ang)
    xf = x.astype(jnp.float32)
    x1, x2 = xf[..., :half], xf[..., half:]
    return jnp.concatenate([x1 * cos - x2 * sin, x2 * cos + x1 * sin], axis=-1).astype(x.dtype)


def masked_softmax(s, mask):
    p = jax.nn.softmax(jnp.where(mask, s, NEG_INF), axis=-1)
    return jnp.where(mask, p, 0.0)


def nsa_attention(q, k_cmp, v_cmp, k_slc, v_slc, k_win, v_win, gates,
                  pos_k, pos_v, ck_w1, ck_w2, cv_w1, cv_w2):
    B, G, HPG, S, D = q.shape
    scale = D ** -0.5
    n_cmp = (S - CMP_LEN) // CMP_STRIDE + 1
    cmp_idx = np.arange(n_cmp)[:, None] * CMP_STRIDE + np.arange(CMP_LEN)[None, :]
    cmp_end = jnp.asarray(cmp_idx[:, -1])

    def compress(tok, pos_emb, w1, w2):
        blk = (tok[:, :, cmp_idx] + pos_emb).reshape(B, G, n_cmp, CMP_LEN * D)
        return jax.nn.silu(blk @ w1) @ w2

    kc = compress(k_cmp, pos_k, ck_w1, ck_w2)
    vc = compress(v_cmp, pos_v, cv_w1, cv_w2)

    n_sel = S // SEL_BLOCK
    top = min(SEL_TOPK, n_sel)
    c_start = cmp_idx[:, 0]
    s_start = np.arange(n_sel) * SEL_BLOCK
    overlap = jnp.asarray((c_start[:, None] < s_start[None, :] + SEL_BLOCK)
                          & (c_start[:, None] + CMP_LEN > s_start[None, :]), dtype=jnp.float32)
    ks_blk = k_slc.reshape(B, G, n_sel, SEL_BLOCK, D)
    vs_blk = v_slc.reshape(B, G, n_sel, SEL_BLOCK, D)
    kw_pad = jnp.pad(k_win, ((0, 0), (0, 0), (WINDOW, 0), (0, 0)))
    vw_pad = jnp.pad(v_win, ((0, 0), (0, 0), (WINDOW, 0), (0, 0)))
    b_ix = jnp.arange(B)[:, None, None, None]
    g_ix = jnp.arange(G)[None, :, None, None]
    sel_j = jnp.arange(n_sel)

    def block(i):
        q0 = i * Q_BLOCK
        t = q0 + jnp.arange(Q_BLOCK)
        qb = lax.dynamic_slice_in_dim(q, q0, Q_BLOCK, axis=3)
        gb = lax.dynamic_slice_in_dim(gates, q0, Q_BLOCK, axis=3)
        s_c = jnp.einsum('bghtd,bgnd->bghtn', qb, kc).astype(jnp.float32) * scale
        p_c = masked_softmax(s_c, cmp_end[None, :] <= t[:, None])
        o_c = jnp.einsum('bghtn,bgnd->bghtd', p_c.astype(vc.dtype), vc)
        imp = jnp.einsum('bghtn,nj->bgtj', p_c, overlap)
        cur = (t // SEL_BLOCK)[:, None]
        forced = (sel_j == 0) | (sel_j == cur) | (sel_j == cur - 1)
        valid = sel_j * SEL_BLOCK <= t[:, None]
        imp = jnp.where(valid, jnp.where(forced, SEL_FORCE, imp), -SEL_FORCE)
        _, top_idx = lax.top_k(imp, top)
        k_g = ks_blk[b_ix, g_ix, top_idx].reshape(B, G, Q_BLOCK, top * SEL_BLOCK, D)
        v_g = vs_blk[b_ix, g_ix, top_idx].reshape(B, G, Q_BLOCK, top * SEL_BLOCK, D)
        tok = (top_idx[..., None] * SEL_BLOCK + jnp.arange(SEL_BLOCK)).reshape(B, G, Q_BLOCK, top * SEL_BLOCK)
        m_s = (tok <= t[:, None])[:, :, None]
        s_s = jnp.einsum('bghtd,bgtkd->bghtk', qb, k_g).astype(jnp.float32) * scale
        p_s = masked_softmax(s_s, m_s)
        o_s = jnp.einsum('bghtk,bgtkd->bghtd', p_s.astype(v_g.dtype), v_g)
        kwb = lax.dynamic_slice_in_dim(kw_pad, q0, Q_BLOCK + WINDOW, axis=2)
        vwb = lax.dynamic_slice_in_dim(vw_pad, q0, Q_BLOCK + WINDOW, axis=2)
        s_pos = q0 - WINDOW + jnp.arange(Q_BLOCK + WINDOW)
        m_w = (s_pos[None, :] <= t[:, None]) & (s_pos[None, :] > t[:, None] - WINDOW) & (s_pos[None, :] >= 0)
        s_w = jnp.einsum('bghtd,bgsd->bghts', qb, kwb).astype(jnp.float32) * scale
        p_w = masked_softmax(s_w, m_w)
        o_w = jnp.einsum('bghts,bgsd->bghtd', p_w.astype(vwb.dtype), vwb)
        return gb[..., 0:1] * o_c + gb[..., 1:2] * o_s + gb[..., 2:3] * o_w

    out = lax.map(block, jnp.arange(S // Q_BLOCK))
    return out.transpose(1, 0, 4, 2, 3, 5).reshape(B, S, G * HPG * D)


def stick_breaking_attention(q, k, v):
    B, H, S, D = q.shape
    scale = D ** -0.5
    s_pos = jnp.arange(S)

    def block(i):
        q0 = i * Q_BLOCK
        t = q0 + jnp.arange(Q_BLOCK)
        qb = lax.dynamic_slice_in_dim(q, q0, Q_BLOCK, axis=2)
        z = jnp.einsum('bhtd,bhsd->bhts', qb, k).astype(jnp.float32) * scale
        mask = s_pos[None, :] < t[:, None]
        log_1mb = jnp.where(mask, jax.nn.log_sigmoid(-z), 0.0)
        rev = lax.cumsum(log_1mb, axis=3, reverse=True)
        between = jnp.concatenate([rev[..., 1:], jnp.zeros_like(rev[..., :1])], axis=-1)
        a = jnp.where(mask, jnp.exp(jax.nn.log_sigmoid(z) + between), 0.0)
        return jnp.einsum('bhts,bhsd->bhtd', a.astype(v.dtype), v)

    out = lax.map(block, jnp.arange(S // Q_BLOCK))
    return out.transpose(1, 0, 3, 2, 4).reshape(B, S, H * D)


def causal_attention(q, k, v):
    B, H, S, Dk = q.shape
    scale = Dk ** -0.5
    s_pos = jnp.arange(S)

    def block(i):
        q0 = i * Q_BLOCK
        t = q0 + jnp.arange(Q_BLOCK)
        qb = lax.dynamic_slice_in_dim(q, q0, Q_BLOCK, axis=2)
        s = jnp.einsum('bhtd,bhsd->bhts', qb, k).astype(jnp.float32) * scale
        p = masked_softmax(s, s_pos[None, :] <= t[:, None])
        return jnp.einsum('bhts,bhsd->bhtd', p.astype(v.dtype), v)

    out = lax.map(block, jnp.arange(S // Q_BLOCK))
    return out.transpose(1, 0, 3, 2, 4).reshape(B, S, H * v.shape[-1])


def hybrid_mixer(h, positions, w_in, pos_k, pos_v, ck_w1, ck_w2, cv_w1, cv_w2,
                 q_norm, w_uq, kv_norm, w_ukv, w_out):
    B, S, _ = h.shape
    (n_q, n_kc, n_vc, n_ks, n_vs, n_kw, n_vw, n_gate, n_z,
     s_q, s_k, s_v, s_z, m_cq, m_ckv, m_kr, m_z) = jnp.split(h @ w_in, IN_SPLITS, axis=-1)

    def heads(t, n):
        return t.reshape(B, S, n, -1).transpose(0, 2, 1, 3)

    q_a = rope(n_q.reshape(B, S, NSA_KV_GROUPS, NSA_HPG, HEAD_DIM).transpose(0, 2, 3, 1, 4), positions)
    gates = jax.nn.sigmoid(n_gate.reshape(B, S, NSA_KV_GROUPS, NSA_HPG, 3).transpose(0, 2, 3, 1, 4))
    o_a = nsa_attention(q_a,
                        rope(heads(n_kc, NSA_KV_GROUPS), positions), heads(n_vc, NSA_KV_GROUPS),
                        rope(heads(n_ks, NSA_KV_GROUPS), positions), heads(n_vs, NSA_KV_GROUPS),
                        rope(heads(n_kw, NSA_KV_GROUPS), positions), heads(n_vw, NSA_KV_GROUPS),
                        gates, pos_k, pos_v, ck_w1, ck_w2, cv_w1, cv_w2) * jax.nn.silu(n_z)

    o_b = stick_breaking_attention(heads(s_q, SB_HEADS), heads(s_k, SB_HEADS),
                                   heads(s_v, SB_HEADS)) * jax.nn.silu(s_z)

    qh = heads(rmsnorm(m_cq, q_norm) @ w_uq, MLA_HEADS)
    kvh = heads(rmsnorm(m_ckv, kv_norm) @ w_ukv, MLA_HEADS)
    k_pe = rope(m_kr[:, None], positions)
    q_c = jnp.concatenate([qh[..., :MLA_NOPE], rope(qh[..., MLA_NOPE:], positions)], axis=-1)
    k_c = jnp.concatenate([kvh[..., :MLA_NOPE], jnp.broadcast_to(k_pe, (B, MLA_HEADS, S, MLA_ROPE))], axis=-1)
    o_c = causal_attention(q_c, k_c, kvh[..., MLA_NOPE:]) * jax.nn.silu(m_z)

    return jnp.concatenate([o_a, o_b, o_c], axis=-1) @ w_out


def setup_inputs(seed: int = 0) -> dict:
    key = jax.random.key(seed)
    ks = jax.random.split(key, 24)

    def nrm(k, shape, scale):
        return jax.random.normal(k, shape, jnp.float32) * scale

    x = nrm(ks[0], (BATCH, SEQ, D_MODEL), 1.0)
    c = nrm(ks[1], (BATCH, D_MODEL), 1.0)
    start = jax.random.randint(ks[2], (BATCH, 1), 0, 2048, dtype=jnp.int32)
    positions = (start + jnp.arange(SEQ, dtype=jnp.int32)[None, :]).astype(jnp.int32)
    return {
        'x': x,
        'c': c,
        'positions': positions,
        'ada_w': nrm(ks[3], (DEPTH, D_MODEL, 3 * D_MODEL), 0.5 * D_MODEL ** -0.5),
        'ada_b': nrm(ks[4], (DEPTH, 3 * D_MODEL), 0.01),
        'norm_g': 1.0 + nrm(ks[5], (DEPTH, D_MODEL), 0.01),
        'w_in': nrm(ks[6], (DEPTH, D_MODEL, D_IN), D_MODEL ** -0.5),
        'nsa_pos_k': nrm(ks[7], (DEPTH, CMP_LEN, HEAD_DIM), 0.1),
        'nsa_pos_v': nrm(ks[8], (DEPTH, CMP_LEN, HEAD_DIM), 0.1),
        'nsa_ck_w1': nrm(ks[9], (DEPTH, CMP_LEN * HEAD_DIM, CMP_HIDDEN), (CMP_LEN * HEAD_DIM) ** -0.5),
        'nsa_ck_w2': nrm(ks[10], (DEPTH, CMP_HIDDEN, HEAD_DIM), CMP_HIDDEN ** -0.5),
        'nsa_cv_w1': nrm(ks[11], (DEPTH, CMP_LEN * HEAD_DIM, CMP_HIDDEN), (CMP_LEN * HEAD_DIM) ** -0.5),
        'nsa_cv_w2': nrm(ks[12], (DEPTH, CMP_HIDDEN, HEAD_DIM), CMP_HIDDEN ** -0.5),
        'mla_q_norm': 1.0 + nrm(ks[13], (DEPTH, MLA_Q_RANK), 0.01),
        'mla_w_uq': nrm(ks[14], (DEPTH, MLA_Q_RANK, MLA_HEADS * (MLA_NOPE + MLA_ROPE)), MLA_Q_RANK ** -0.5),
        'mla_kv_norm': 1.0 + nrm(ks[15], (DEPTH, MLA_KV_RANK), 0.01),
        'mla_w_ukv': nrm(ks[16], (DEPTH, MLA_KV_RANK, MLA_HEADS * (MLA_NOPE + MLA_V)), MLA_KV_RANK ** -0.5),
        'w_out': nrm(ks[17], (DEPTH, D_MIX, D_MODEL), D_MIX ** -0.5),
        'final_norm': 1.0 + nrm(ks[18], (D_MODEL,), 0.01),
    }


def reference(x, c, positions, ada_w, ada_b, norm_g, w_in, nsa_pos_k, nsa_pos_v,
              nsa_ck_w1, nsa_ck_w2, nsa_cv_w1, nsa_cv_w2, mla_q_norm, mla_w_uq,
              mla_kv_norm, mla_w_ukv, w_out, final_norm):
    for l in range(DEPTH):
        mod = jax.nn.silu(c) @ ada_w[l] + ada_b[l]
        shift, scale, gate = jnp.split(mod, 3, axis=-1)
        h = rmsnorm(x, norm_g[l]) * (1.0 + scale[:, None, :]) + shift[:, None, :]
        y = hybrid_mixer(h, positions, w_in[l], nsa_pos_k[l], nsa_pos_v[l],
                         nsa_ck_w1[l], nsa_ck_w2[l], nsa_cv_w1[l], nsa_cv_w2[l],
                         mla_q_norm[l], mla_w_uq[l], mla_kv_norm[l], mla_w_ukv[l], w_out[l])
        x = x + gate[:, None, :] * y
    return rmsnorm(x, final_norm)
```

```python
import numpy as np
import concourse.bass as bass
import concourse.mybir as mybir
from concourse.bass_utils import run_bass_kernel_spmd

F32 = mybir.dt.float32
BF16 = mybir.dt.bfloat16
I32 = mybir.dt.int32
AF = mybir.ActivationFunctionType
ALU = mybir.AluOpType
AX = mybir.AxisListType

ENGS = ['pe', 'act', 'dve', 'pool', 'sp']
SEM_CH = 8192
N_DMA_SEMS = 24

S_LEN = 4096
NT = 32
D = 1024
DEPTH = 2
D_IN = 3506
EPS = 1e-6
PI = float(np.pi)

P_R64, P_VC, P_VS, P_VW, P_SQ, P_SK, P_SV, P_CQ, P_CKV, P_KR, P_Z, P_GATE = (
    0, 768, 896, 1024, 1152, 1472, 1792, 2112, 2304, 2432, 2464, 3488)


class Sched:
    def __init__(self, nc):
        self.nc = nc
        self.ops = {e: [] for e in ENGS}
        self.cnt = {e: 0 for e in ENGS}
        self.waited = {e: {} for e in ENGS}
        self.waited_dma = {e: set() for e in ENGS}
        self.sems = {}
        self.dma_sems = [nc.alloc_semaphore(f"dsem{i}") for i in range(N_DMA_SEMS)]
        self.dma_uses = [0] * N_DMA_SEMS
        self.dma_last = [None] * N_DMA_SEMS
        self.dma_rr = 0
        self.lastw = {}
        self.readers = {}
        self.out_tokens = []

    def _sem(self, eng, idx):
        ch = (idx - 1) // SEM_CH
        k = (eng, ch)
        if k not in self.sems:
            self.sems[k] = self.nc.alloc_semaphore(f"s_{eng}_{ch}")
        return self.sems[k], (idx - 1) % SEM_CH + 1

    def _need(self, eng, tok, waits):
        if tok is None:
            return
        if tok[0] == 'dma':
            if tok in self.waited_dma[eng]:
                return
            self.waited_dma[eng].add(tok)
            waits.append((self.dma_sems[tok[1]], tok[2]))
        else:
            f, idx = tok
            if f == eng and eng in ('pe', 'sp'):
                return
            if self.waited[eng].get(f, 0) >= idx:
                return
            self.waited[eng][f] = idx
            waits.append(self._sem(f, idx))

    def _deps(self, eng, reads, writes):
        waits = []
        for k in reads:
            self._need(eng, self.lastw.get(k), waits)
            if k.startswith('ps'):
                for t in self.readers.get(k, ()):
                    if t[0] != eng:
                        self._need(eng, t, waits)
        for k in writes:
            self._need(eng, self.lastw.get(k), waits)
            for t in self.readers.get(k, ()):
                self._need(eng, t, waits)
        return waits

    def _commit(self, tok, reads, writes):
        for k in reads:
            self.readers.setdefault(k, []).append(tok)
        for k in writes:
            self.lastw[k] = tok
            self.readers[k] = []

    def op(self, eng, fn, reads=(), writes=()):
        waits = self._deps(eng, reads, writes)
        self.cnt[eng] += 1
        idx = self.cnt[eng]
        sem, val = self._sem(eng, idx)
        self.ops[eng].append((fn, waits, sem, 1))
        self._commit((eng, idx), reads, writes)

    def dma(self, out, in_, reads=(), writes=(), eng='sp', is_output=False, slow=False):
        waits = self._deps(eng, reads, writes)
        slot = self.dma_rr
        self.dma_rr = (self.dma_rr + 1) % N_DMA_SEMS
        self._need(eng, self.dma_last[slot], waits)
        self.dma_uses[slot] += 1
        tok = ('dma', slot, 16 * self.dma_uses[slot])
        self.dma_last[slot] = tok
        self.ops[eng].append((lambda e: e.dma_start(out=out, in_=in_, allow_slow_non_contiguous=slow), waits, self.dma_sems[slot], 16))
        self._commit(tok, reads, writes)
        if is_output:
            self.out_tokens.append(tok)

    def barrier(self):
        for e in ENGS:
            waits = []
            for f in ENGS:
                if f not in ('sp',) and self.cnt[f] > 0:
                    self._need(e, (f, self.cnt[f]), waits)
            for s in range(N_DMA_SEMS):
                self._need(e, self.dma_last[s], waits)
            if waits:
                self.ops[e].append((None, waits, None, 0))
        self.lastw = {}
        self.readers = {}

    def emit(self):
        nc = self.nc
        waits = []
        for t in self.out_tokens:
            self._need('sp', t, waits)
        for s in range(N_DMA_SEMS):
            self._need('sp', self.dma_last[s], waits)
        for e in ENGS:
            if e != 'sp' and self.cnt[e] > 0:
                self._need('sp', (e, self.cnt[e]), waits)
        self.ops['sp'].append((None, waits, None, 0))
        with nc.Block() as block:
            def run(e):
                def body(eng):
                    for fn, waits, sem, inc in self.ops[e]:
                        for (s, v) in waits:
                            eng.wait_ge(s, v)
                        if fn is not None:
                            ins = fn(eng)
                            ins.then_inc(sem, inc)
                return body
            block.tensor(run('pe'))
            block.scalar(run('act'))
            block.vector(run('dve'))
            block.gpsimd(run('pool'))
            block.sync(run('sp'))


def fap(base, dims):
    return bass.AP(tensor=base.tensor, offset=base.offset, ap=[list(base.ap[0])] + [list(d) for d in dims])


class SB:
    def __init__(self, nc):
        self.nc = nc
        self.off = 17408
        self.n = 0

    def alloc(self, name, shape, dtype):
        esz = 4 if dtype in (F32, I32) else 2
        free = int(np.prod(shape[1:])) * esz
        self.off = (self.off + 63) // 64 * 64
        self.n += 1
        t = self.nc.alloc_sbuf_tensor_at(f"{name}_{self.n}", list(shape), dtype, offset=self.off)
        self.off += free
        assert self.off <= 228 * 1024 - 512, (name, self.off)
        return t


def host_consts():
    c = {}
    c['ident'] = np.eye(128, dtype=np.float32)
    k = np.arange(128)[:, None]
    q = np.arange(512)[None, :]
    c['m_causal'] = (np.stack([((128 * i + k) <= q) for i in range(4)], 1).astype(np.float32) - 1.0) * 30000.0
    c['m_strict'] = (np.stack([((128 * i + k) < q) for i in range(4)], 1).astype(np.float32) - 1.0) * 30000.0
    c['m_win'] = (np.stack([(((128 * o + k) <= q) & ((128 * o + k) > q - 512)) for o in range(-4, 4)], 1).astype(np.float32) - 1.0) * 30000.0
    c['tri'] = (k >= np.arange(128)[None, :]).astype(np.float32)
    qq = np.arange(128)[None, :]
    c['mtri'] = (np.concatenate([(k <= qq), (k < qq), (k > qq)], axis=1).astype(np.float32) - 1.0) * 30000.0
    inv64 = (10000.0 ** (-np.arange(32, dtype=np.float32) / 32)).astype(np.float32)
    inv32 = (10000.0 ** (-np.arange(16, dtype=np.float32) / 16)).astype(np.float32)
    c['inv'] = np.broadcast_to(np.concatenate([inv64, inv32])[None, :], (128, 48)).astype(np.float32).copy()
    m = np.arange(503)[None, :]
    c['mbig'] = ((16 * (m - 248) + 31) <= k).astype(np.float32)
    r = np.arange(126)[None, :] - 62
    cb = k // 64
    invalid = r > cb
    f_cur = r == cb
    f_prev = r == cb - 1
    keep = (~invalid & ~f_cur & ~f_prev).astype(np.float32)
    add = np.where(invalid, -1e9, 0.0) + np.where(f_cur, 1.0e9, 0.0) + np.where(f_prev, 1.1e9, 0.0)
    c['selkeep'] = keep
    c['seladd'] = add.astype(np.float32)
    oh = np.zeros((128, S_LEN), np.float32)
    oh[64 + np.arange(S_LEN) // 64, np.arange(S_LEN)] = 1.0
    c['onehot'] = oh
    return c


CONST_SHAPES = {'ident': [128, 128], 'm_causal': [128, 4, 512], 'm_strict': [128, 4, 512], 'm_win': [128, 8, 512],
                'tri': [128, 128], 'mtri': [128, 384], 'inv': [128, 48], 'mbig': [128, 503], 'selkeep': [128, 126], 'seladd': [128, 126],
                'onehot': [128, S_LEN]}

IN_SHAPES = {
    'x': ([S_LEN, D], F32), 'c_row': ([1, D], F32), 'pos_pt': ([128, NT], I32),
    'ada_wT': ([DEPTH, 3 * D, D], F32), 'ada_bc': ([DEPTH, 128, 24], F32), 'norm_g': ([DEPTH, 128, 8], F32),
    'w_in': ([DEPTH, D, D_IN], F32), 'pos_kT': ([DEPTH, 64, 32], F32), 'pos_vT': ([DEPTH, 64, 32], F32),
    'ck_w1': ([DEPTH, 2048, 128], F32), 'ck_w2': ([DEPTH, 128, 64], F32),
    'cv_w1': ([DEPTH, 2048, 128], F32), 'cv_w2': ([DEPTH, 128, 64], F32),
    'q_norm': ([DEPTH, 192, 1], F32), 'w_uq': ([DEPTH, 192, 480], F32),
    'kv_norm': ([DEPTH, 128, 1], F32), 'w_ukv': ([DEPTH, 128, 640], F32),
    'w_out': ([DEPTH, D, D], F32), 'final_norm': ([1, D], F32),
}


def prep_inputs(inp):
    w = {}
    sp = np.cumsum([0, 384, 128, 128, 128, 128, 128, 128, 18, 384, 320, 320, 320, 320, 192, 128, 32, 320])
    names = ['n_q', 'n_kc', 'n_vc', 'n_ks', 'n_vs', 'n_kw', 'n_vw', 'n_gate', 'n_z', 's_q', 's_k', 's_v', 's_z',
             'm_cq', 'm_ckv', 'm_kr', 'm_z']
    rng = {n: np.arange(sp[i], sp[i + 1]) for i, n in enumerate(names)}
    order = ['n_q', 'n_kc', 'n_ks', 'n_kw', 'n_vc', 'n_vs', 'n_vw', 's_q', 's_k', 's_v', 'm_cq', 'm_ckv', 'm_kr',
             'n_z', 's_z', 'm_z', 'n_gate']
    perm = np.concatenate([rng[n] for n in order])
    shared = {
        'ada_wT': np.ascontiguousarray(inp['ada_w'].transpose(0, 2, 1)),
        'ada_bc': np.ascontiguousarray(inp['ada_b'].reshape(DEPTH, 24, 128).transpose(0, 2, 1)),
        'norm_g': np.ascontiguousarray(inp['norm_g'].reshape(DEPTH, 8, 128).transpose(0, 2, 1)),
        'w_in': np.ascontiguousarray(inp['w_in'][:, :, perm]),
        'pos_kT': np.ascontiguousarray(inp['nsa_pos_k'].transpose(0, 2, 1)),
        'pos_vT': np.ascontiguousarray(inp['nsa_pos_v'].transpose(0, 2, 1)),
        'ck_w1': np.ascontiguousarray(inp['nsa_ck_w1']), 'ck_w2': np.ascontiguousarray(inp['nsa_ck_w2']),
        'cv_w1': np.ascontiguousarray(inp['nsa_cv_w1']), 'cv_w2': np.ascontiguousarray(inp['nsa_cv_w2']),
        'q_norm': np.ascontiguousarray(inp['mla_q_norm'].reshape(DEPTH, 192, 1)),
        'kv_norm': np.ascontiguousarray(inp['mla_kv_norm'].reshape(DEPTH, 128, 1)),
        'w_out': np.ascontiguousarray(inp['w_out']),
        'final_norm': np.ascontiguousarray(inp['final_norm'].reshape(1, D)),
    }
    uq = inp['mla_w_uq'].reshape(DEPTH, 192, 5, 96)
    shared['w_uq'] = np.ascontiguousarray(np.concatenate(
        [uq[..., :64].reshape(DEPTH, 192, 320), uq[..., 64:].reshape(DEPTH, 192, 160)], axis=-1))
    ukv = inp['mla_w_ukv'].reshape(DEPTH, 128, 5, 128)
    shared['w_ukv'] = np.ascontiguousarray(np.concatenate(
        [ukv[..., :64].reshape(DEPTH, 128, 320), ukv[..., 64:].reshape(DEPTH, 128, 320)], axis=-1))
    shared.update(host_consts())
    maps = []
    for b in range(8):
        m = dict(shared)
        m['x'] = np.ascontiguousarray(inp['x'][b])
        m['c_row'] = np.ascontiguousarray(inp['c'][b].reshape(1, D))
        m['pos_pt'] = np.ascontiguousarray(inp['positions'][b].reshape(NT, 128).T.astype(np.int32))
        maps.append(m)
    return maps


def phase_mod(g, l):
    S, sb, IN, DR = g.S, g.sb, g.IN, g.DR
    S.barrier()
    sb.off = g.persist_mark
    scb = sb.alloc("scb", [128, D], F32)
    gcol = sb.alloc("gcol", [128, 8], F32)
    bcol = sb.alloc("bcol", [128, 24], F32)
    modc = sb.alloc("modc", [128, 24], F32)
    aw = [sb.alloc(f"aw{i}", [128, D], F32) for i in range(2)]
    jf = sb.alloc("jf", [128, D], F32)
    S.dma(scb[:, :], bass.AP(tensor=IN['c_row'].tensor, offset=0, ap=[[0, 128], [1, D]]), writes=['scb'])
    S.dma(gcol[:, :], IN['norm_g'][l], writes=['gcol'])
    S.dma(bcol[:, :], IN['ada_bc'][l], writes=['bcol'])
    S.op('act', lambda e: e.activation(out=scb[:, :], in_=scb[:, :], func=AF.Silu), reads=['scb'], writes=['scb'])
    S.op('dve', lambda e: e.memset(modc[:, :], 0.0), writes=['modc'])
    for ch in range(24):
        b = ch % 2
        S.dma(aw[b][:, :], IN['ada_wT'][l, ch * 128:(ch + 1) * 128, :], writes=[f'aw{b}'])
        S.op('dve', lambda e, b=b, ch=ch: e.scalar_tensor_tensor(out=jf[:, :], in0=aw[b][:, :], scalar=1.0, in1=scb[:, :], op0=ALU.mult, op1=ALU.mult,
                                                               accum_out=modc[:, ch:ch + 1]),
             reads=[f'aw{b}', 'scb', 'modc'], writes=['jf', f'modc{ch}'])
    mk = [f'modc{ch}' for ch in range(24)]
    S.op('dve', lambda e: e.tensor_tensor(out=modc[:, :], in0=modc[:, :], in1=bcol[:, :], op=ALU.add), reads=mk + ['bcol'], writes=['modc'])
    S.op('dve', lambda e: e.tensor_copy(out=g.modS[:, :], in_=modc[:, 0:8]), reads=['modc'], writes=['modS'])
    S.op('dve', lambda e: e.scalar_tensor_tensor(out=g.modA[:, :], in0=modc[:, 8:16], scalar=1.0, in1=gcol[:, :], op0=ALU.add, op1=ALU.mult),
         reads=['modc', 'gcol'], writes=['modA'])
    S.dma(DR['GROW'].rearrange("o (c p) -> p (o c)", p=128), modc[:, 16:24], reads=['modc'], writes=['GROW'], slow=True)
    S.dma(g.gate_b[:, :], bass.AP(tensor=DR['GROW'].tensor, offset=0, ap=[[0, 128], [1, D]]), reads=['GROW'], writes=['gate_b'])


def rope_ops(S, engs, x1, x2, cos, sin, o1, o2, t, rk, wk, tk):
    ea, eb = engs
    S.op(ea, lambda e: e.tensor_tensor(out=t[0], in0=x1, in1=cos, op=ALU.mult), reads=rk, writes=[tk[0]])
    S.op(ea, lambda e: e.tensor_tensor(out=t[1], in0=x2, in1=sin, op=ALU.mult), reads=rk, writes=[tk[1]])
    S.op(ea, lambda e: e.tensor_tensor(out=o1, in0=t[0], in1=t[1], op=ALU.subtract), reads=[tk[0], tk[1]], writes=[wk + '.o1'])
    S.op(eb, lambda e: e.tensor_tensor(out=t[2], in0=x2, in1=cos, op=ALU.mult), reads=rk, writes=[tk[2]])
    S.op(eb, lambda e: e.tensor_tensor(out=t[3], in0=x1, in1=sin, op=ALU.mult), reads=rk, writes=[tk[3]])
    S.op(eb, lambda e: e.tensor_tensor(out=o2, in0=t[2], in1=t[3], op=ALU.add), reads=[tk[2], tk[3]], writes=[wk + '.o2'])


def phase_inproj(g, l):
    import os
    CUT = int(os.environ.get('KCUT', '99'))
    NTT = int(os.environ.get('KNT', str(NT)))
    S, sb, IN, DR, C, ps, psb = g.S, g.sb, g.IN, g.DR, g.C, g.ps, g.psb
    S.barrier()
    sb.off = g.persist_mark
    W = sb.alloc("W", [128, 8, D_IN], BF16)
    proj = [sb.alloc(f"proj{i}", [128, D_IN], F32) for i in range(2)]
    Wuq = sb.alloc("Wuq", [128, 2, 480], BF16)
    Wukv = sb.alloc("Wukv", [128, 640], BF16)
    nrm = sb.alloc("nrm", [128, 3], F32)
    xt = [sb.alloc(f"xt{i}", [128, D], F32) for i in range(2)]
    junk = sb.alloc("junk", [128, D], BF16)
    xn = [sb.alloc(f"xn{i}", [128, D], BF16) for i in range(2)]
    hT = [sb.alloc(f"hT{i}", [128, 8, 128], BF16) for i in range(2)]
    htmp = sb.alloc("htmp", [128, 8, 128], F32)
    rt = [sb.alloc(f"rt{i}", [128, 12, 32], F32) for i in range(4)]
    TS = [sb.alloc(f"TS{i}", [128, 1984], BF16) for i in range(2)]
    FT = [sb.alloc(f"FT{i}", [128, 16, 128], BF16) for i in range(2)]
    vsw = [sb.alloc(f"vsw{i}", [128, 256], BF16) for i in range(2)]
    svt = [sb.alloc(f"svt{i}", [128, 320], BF16) for i in range(2)]
    zt = [sb.alloc(f"zt{i}", [128, 1024], F32) for i in range(2)]
    graw = sb.alloc("graw", [128, NT, 18], F32)
    ss = sb.alloc("ss", [128, NT, 3], F32)
    rs = sb.alloc("rs", [128, NT, 3], F32)
    qnb = [sb.alloc(f"qnb{i}", [128, 3, 128], BF16) for i in range(2)]
    knb = [sb.alloc(f"knb{i}", [128, 3, 128], BF16) for i in range(2)]
    mvb = [sb.alloc(f"mvb{i}", [128, 320], BF16) for i in range(2)]
    qrb = [sb.alloc(f"qrb{i}", [128, 256], BF16) for i in range(2)]
    qrT = [sb.alloc(f"qrT{i}", [128, 2, 128], BF16) for i in range(2)]
    ident = C['ident']
    cos64, sin64 = g.tabs['64']
    cos32, sin32 = g.tabs['32']

    for k in range(8):
        b = k % 2
        S.dma(proj[b][:, :], IN['w_in'][l, k * 128:(k + 1) * 128, :], writes=[f'proj{b}'] + [f'proj{b}.{cc}' for cc in range(7)])
        if k % 3 == 2:
            S.op('act', lambda e, b=b, k=k: e.copy(out=W[:, k, :], in_=proj[b][:, :]), reads=[f'proj{b}'], writes=[f'W{k}'])
        else:
            S.op(['dve', 'pool'][k % 3], lambda e, b=b, k=k: e.tensor_copy(out=W[:, k, :], in_=proj[b][:, :]), reads=[f'proj{b}'], writes=[f'W{k}'])
    S.dma(nrm[:, 0:1], IN['q_norm'][l, 0:128, :], writes=['nrm'])
    S.dma(nrm[0:64, 1:2], IN['q_norm'][l, 128:192, :], writes=['nrm'])
    S.dma(nrm[:, 2:3], IN['kv_norm'][l], writes=['nrm'])
    S.dma(xt[0][:, 0:480], IN['w_uq'][l, 0:128, :], writes=['xt0'])
    S.dma(xt[0][0:64, 480:960], IN['w_uq'][l, 128:192, :], writes=['xt0'])
    S.dma(xt[1][:, 0:640], IN['w_ukv'][l], writes=['xt1'])
    S.op('dve', lambda e: e.tensor_scalar(out=Wuq[:, 0, :], in0=xt[0][:, 0:480], scalar1=nrm[:, 0:1], scalar2=None, op0=ALU.mult),
         reads=['xt0', 'nrm'], writes=['Wuq'])
    S.op('dve', lambda e: e.tensor_scalar(out=Wuq[0:64, 1, :], in0=xt[0][0:64, 480:960], scalar1=nrm[0:64, 1:2], scalar2=None, op0=ALU.mult),
         reads=['xt0', 'nrm'], writes=['Wuq'])
    S.op('dve', lambda e: e.tensor_scalar(out=Wukv[:, :], in0=xt[1][:, 0:640], scalar1=nrm[:, 2:3], scalar2=None, op0=ALU.mult),
         reads=['xt1', 'nrm'], writes=['Wukv'])
    S.op('dve', lambda e: e.memset(ss[:, :, :], 0.0), writes=['ss'])
    for i_ in range(2):
        S.op('pool', lambda e, i_=i_: e.memset(TS[i_][:, 1888:1984], 0.0), writes=[f'TS{i_}.pad'])
        S.op('pool', lambda e, i_=i_: e.memset(qrb[i_][:, 160:256], 0.0), writes=[f'qrb{i_}.pad'])
    WK = [f'W{k}' for k in range(8)]
    xsrc = IN['x'] if l == 0 else DR['X1']
    chunks = [(c0, min(512, D_IN - c0)) for c0 in range(0, D_IN, 512)]

    rtk = [sb.alloc(f"rtk{i}", [128, 16], F32) for i in range(4)]
    rtq = [sb.alloc(f"rtq{i}", [128, 5, 16], F32) for i in range(4)]
    mhalf = sb.alloc("mhalf", [128, 4], F32)
    S.op('pool', lambda e: e.memset(mhalf[:, :], -0.5), writes=['mhalf'])

    def st1a(tt):
        b = tt % 2
        tts = slice(tt * 128, (tt + 1) * 128)
        S.dma(xt[b][:, :], xsrc[tts, :], writes=[f'xt{b}'], eng='pool')
        S.op('act', lambda e: e.activation(out=junk[:, :], in_=xt[b][:, :], func=AF.Square, accum_out=ss[:, tt, 0:1]),
             reads=[f'xt{b}', 'ss'], writes=['junk', f'ss{tt}.0'])
        S.op('dve', lambda e: e.tensor_scalar(out=rs[:, tt, 0:1], in0=ss[:, tt, 0:1], scalar1=1.0 / D, scalar2=EPS, op0=ALU.mult, op1=ALU.add),
             reads=[f'ss{tt}.0'], writes=[f'rs{tt}.0'])
        S.op('pool', lambda e: e.tensor_tensor(out=rs[:, tt, 0:1], in0=rs[:, tt, 0:1], in1=mhalf[:, 0:1], op=ALU.pow), reads=[f'rs{tt}.0', 'mhalf'], writes=[f'rs{tt}.0'])
        S.op('act', lambda e: e.activation(out=xn[b][:, :], in_=xt[b][:, :], func=AF.Copy, scale=rs[:, tt, 0:1]),
             reads=[f'xt{b}', f'rs{tt}.0'], writes=[f'xn{b}'])

    def st1b(tt):
        b = tt % 2
        for c in range(8):
            S.op('pe', lambda e, c=c: e.transpose(psb[7][:, c * 128:(c + 1) * 128], xn[b][:, c * 128:(c + 1) * 128], ident[:, :]),
                 reads=[f'xn{b}', 'c_ident'], writes=['ps7'])
        S.op('dve', lambda e: e.tensor_tensor(out=htmp[:, :, :], in0=psb[7][:, :].rearrange("p (c t) -> p c t", t=128),
                                              in1=fap(g.modA[:, :], [(1, 8), (0, 128)]), op=ALU.mult),
             reads=['ps7', 'modA'], writes=['htmp'])
        S.op('dve', lambda e: e.tensor_tensor(out=hT[b][:, :, :], in0=htmp[:, :, :], in1=fap(g.modS[:, :], [(1, 8), (0, 128)]), op=ALU.add),
             reads=['htmp', 'modS'], writes=[f'hT{b}'])

    def st2a(tt):
        b = tt % 2
        for cc, (c0, n) in enumerate(chunks):
            bk = cc % 2
            for k in range(8):
                S.op('pe', lambda e, k=k, c0=c0, n=n, bk=bk: e.matmul(ps[bk][:, 0:n], lhsT=hT[b][:, k, :], rhs=W[:, k, c0:c0 + n],
                                                                    start=(k == 0), stop=(k == 7)),
                     reads=[f'hT{b}', WK[k]], writes=[f'ps{bk}'])
            S.op('act', lambda e, c0=c0, n=n, bk=bk: e.copy(out=proj[b][:, c0:c0 + n], in_=ps[bk][:, 0:n]),
                 reads=[f'ps{bk}'], writes=[f'proj{b}.{cc}'])

    def st2b(tt):
        b = tt % 2
        R = proj[b]
        pk = [f'proj{b}.{cc}' for cc in range(7)]
        R3 = R[:, 0:768].rearrange("p (h d) -> p h d", d=64)
        T3 = TS[b][:, 0:768].rearrange("p (h d) -> p h d", d=64)
        cb = fap(cos64[:, tt, :], [(0, 12), (1, 32)])
        snb = fap(sin64[:, tt, :], [(0, 12), (1, 32)])
        rope_ops(S, ('dve', 'pool'), R3[:, :, 0:32], R3[:, :, 32:64], cb, snb, T3[:, :, 0:32], T3[:, :, 32:64],
                 [r[:, :, :] for r in rt], [f'proj{b}.0', f'proj{b}.1', 'cos64', 'sin64'], f'TS{b}.r64', ['rt0', 'rt1', 'rt2', 'rt3'])
        S.op('pool', lambda e: e.tensor_copy(out=TS[b][:, 768:896], in_=proj[b][:, P_VC:P_VC + 128]), reads=pk, writes=[f'TS{b}.vc'])
        S.op('pool', lambda e: e.tensor_copy(out=TS[b][:, 896:1536], in_=proj[b][:, P_SQ:P_SQ + 640]), reads=pk, writes=[f'TS{b}.sqk'])
        S.op('pool', lambda e: e.tensor_copy(out=vsw[b][:, :], in_=proj[b][:, P_VS:P_VS + 256]), reads=pk, writes=[f'vsw{b}'])
        S.op('pool', lambda e: e.tensor_copy(out=svt[b][:, :], in_=proj[b][:, P_SV:P_SV + 320]), reads=pk, writes=[f'svt{b}'])
        S.op('act', lambda e: e.activation(out=zt[b][:, :], in_=proj[b][:, P_Z:P_Z + 1024], func=AF.Silu), reads=pk, writes=[f'zt{b}'])
        S.op('pool', lambda e: e.tensor_copy(out=graw[:, tt, :], in_=proj[b][:, P_GATE:P_GATE + 18]), reads=pk, writes=[f'graw{tt}'])
        S.op('act', lambda e: e.activation(out=junk[:, 0:192], in_=proj[b][:, P_CQ:P_CQ + 192], func=AF.Square, accum_out=ss[:, tt, 1:2]),
             reads=pk + ['ss'], writes=['junk', f'ss{tt}.1'])
        S.op('act', lambda e: e.activation(out=junk[:, 0:128], in_=proj[b][:, P_CKV:P_CKV + 128], func=AF.Square, accum_out=ss[:, tt, 2:3]),
             reads=pk + ['ss'], writes=['junk', f'ss{tt}.2'])
        for j, dn in ((1, 192.0), (2, 128.0)):
            S.op('dve', lambda e, j=j, dn=dn: e.tensor_scalar(out=rs[:, tt, j:j + 1], in0=ss[:, tt, j:j + 1], scalar1=1.0 / dn, scalar2=EPS, op0=ALU.mult, op1=ALU.add),
                 reads=[f'ss{tt}.{j}'], writes=[f'rs{tt}.{j}'])
        S.op('pool', lambda e: e.tensor_tensor(out=rs[:, tt, 1:3], in0=rs[:, tt, 1:3], in1=mhalf[:, 1:3], op=ALU.pow),
             reads=[f'rs{tt}.1', f'rs{tt}.2', 'mhalf'], writes=[f'rs{tt}.1', f'rs{tt}.2'])
        S.op('dve', lambda e: e.tensor_scalar(out=TS[b][:, 1536:1728], in0=proj[b][:, P_CQ:P_CQ + 192], scalar1=rs[:, tt, 1:2], scalar2=None, op0=ALU.mult),
             reads=pk + [f'rs{tt}.1'], writes=[f'TS{b}.cq'])
        S.op('dve', lambda e: e.tensor_scalar(out=TS[b][:, 1728:1856], in0=proj[b][:, P_CKV:P_CKV + 128], scalar1=rs[:, tt, 2:3], scalar2=None, op0=ALU.mult),
             reads=pk + [f'rs{tt}.2'], writes=[f'TS{b}.ckv'])
        rope_ops(S, ('dve', 'dve'), R[:, P_KR:P_KR + 16], R[:, P_KR + 16:P_KR + 32], cos32[:, tt, :], sin32[:, tt, :],
                 TS[b][:, 1856:1872], TS[b][:, 1872:1888], [r[:, :] for r in rtk], pk + ['cos32', 'sin32'], f'TS{b}.kr', ['rtk0', 'rtk1', 'rtk2', 'rtk3'])

    def st3a(tt):
        b = tt % 2
        tts = slice(tt * 128, (tt + 1) * 128)
        tskeys = [f'TS{b}.r64.o1', f'TS{b}.r64.o2', f'TS{b}.vc', f'TS{b}.sqk', f'TS{b}.cq', f'TS{b}.ckv', f'TS{b}.kr.o1', f'TS{b}.kr.o2', f'TS{b}.pad']
        tsrc = [(i * 128, 128) for i in range(12)] + [(1536, 128), (1664, 128), (1728, 128), (1856, 128)]
        for i, (c0, w) in enumerate(tsrc):
            bk = 3 + i // 8
            sl = i % 8
            S.op('pe', lambda e, c0=c0, w=w, bk=bk, sl=sl: e.transpose(psb[bk][0:w, sl * 128:(sl + 1) * 128], TS[b][:, c0:c0 + w], ident[:, :]),
                 reads=tskeys + ['c_ident'], writes=[f'ps{bk}'])
        S.op('dve', lambda e: e.tensor_copy(out=FT[b][:, 0:8, :], in_=psb[3][:, :].rearrange("p (c t) -> p c t", t=128)),
             reads=['ps3'], writes=[f'FT{b}.a'])
        S.op('act', lambda e: e.copy(out=FT[b][:, 8:16, :], in_=psb[4][:, :].rearrange("p (c t) -> p c t", t=128)),
             reads=['ps4'], writes=[f'FT{b}.b'])
        S.dma(DR['R64T'].rearrange("(i p) t -> p i t", p=128)[:, :, tts], FT[b][:, 0:6, :], reads=[f'FT{b}.a'])
        S.dma(DR['VCT'][:, tts], FT[b][:, 6, :], reads=[f'FT{b}.a'])
        S.dma(DR['SQKT'][0:128, tts], FT[b][:, 7, :], reads=[f'FT{b}.a'])
        S.dma(DR['SQKT'].rearrange("(i p) t -> p i t", p=128)[:, 1:5, tts], FT[b][:, 8:12, :], reads=[f'FT{b}.b'])
        S.dma(DR['MKPE'][:, tts], FT[b][0:32, 15, :], reads=[f'FT{b}.b'])
        S.dma(DR['VSW'][tts, :], vsw[b][:, :], reads=[f'vsw{b}'])
        S.dma(DR['SV'][tts, :], svt[b][:, :], reads=[f'svt{b}'])
        S.dma(DR['Z'][tts, :], zt[b][:, :], reads=[f'zt{b}'])

    def st3b(tt):
        b = tt % 2
        tts = slice(tt * 128, (tt + 1) * 128)
        cq0, cq1, ckvT = FT[b][:, 12, :], FT[b][0:64, 13, :], FT[b][:, 14, :]
        fk = [f'FT{b}.b']
        for grp, (c0, w) in enumerate(((0, 128), (128, 128), (256, 128))):
            S.op('pe', lambda e, grp=grp, c0=c0, w=w: e.matmul(ps[5][0:w, grp * 128:(grp + 1) * 128], lhsT=Wuq[:, 0, c0:c0 + w], rhs=cq0, start=True, stop=False),
                 reads=fk + ['Wuq'], writes=['ps5'])
            S.op('pe', lambda e, grp=grp, c0=c0, w=w: e.matmul(ps[5][0:w, grp * 128:(grp + 1) * 128], lhsT=Wuq[0:64, 1, c0:c0 + w], rhs=cq1, start=False, stop=True),
                 reads=fk + ['Wuq'], writes=['ps5'])
            S.op('pe', lambda e, grp=grp, c0=c0, w=w: e.matmul(ps[6][0:w, grp * 128:(grp + 1) * 128], lhsT=Wukv[:, c0:c0 + w], rhs=ckvT, start=True, stop=True),
                 reads=fk + ['Wukv'], writes=['ps6'])
        S.op('pe', lambda e: e.matmul(ps[2][:, 0:160], lhsT=cq0, rhs=Wuq[:, 0, 320:480], start=True, stop=False), reads=fk + ['Wuq'], writes=['ps2'])
        S.op('pe', lambda e: e.matmul(ps[2][:, 0:160], lhsT=cq1, rhs=Wuq[0:64, 1, 320:480], start=False, stop=True), reads=fk + ['Wuq'], writes=['ps2'])
        S.op('pe', lambda e: e.matmul(ps[2][:, 160:480], lhsT=ckvT, rhs=Wukv[:, 320:640], start=True, stop=True, skip_group_check=True),
             reads=fk + ['Wukv'], writes=['ps2'])
        S.op('act', lambda e: e.copy(out=qnb[b][:, :, :], in_=ps[5][:, 0:384].rearrange("p (c t) -> p c t", t=128)), reads=['ps5'], writes=[f'qnb{b}'])
        S.op('dve', lambda e: e.tensor_copy(out=knb[b][:, :, :], in_=ps[6][:, 0:384].rearrange("p (c t) -> p c t", t=128)), reads=['ps6'], writes=[f'knb{b}'])
        S.op('act', lambda e: e.copy(out=mvb[b][:, :], in_=ps[2][:, 160:480]), reads=['ps2'], writes=[f'mvb{b}'])
        Q3 = ps[2][:, 0:160].rearrange("p (h d) -> p h d", d=32)
        O3 = qrb[b][:, 0:160].rearrange("p (h d) -> p h d", d=32)
        rope_ops(S, ('dve', 'dve'), Q3[:, :, 0:16], Q3[:, :, 16:32], fap(cos32[:, tt, :], [(0, 5), (1, 16)]), fap(sin32[:, tt, :], [(0, 5), (1, 16)]),
                 O3[:, :, 0:16], O3[:, :, 16:32], [r[:, :, :] for r in rtq], ['ps2', 'cos32', 'sin32'], f'qrb{b}', ['rtq0', 'rtq1', 'rtq2', 'rtq3'])
        for nm, src in (('MQN', qnb[b]), ('MKN', knb[b])):
            S.dma(DR[nm][0:256, :].rearrange("(i p) t -> p i t", p=128)[:, :, tts], src[:, 0:2, :], reads=[f'{nm[1:].lower()}b{b}'])
            S.dma(DR[nm][256:320, tts], src[0:64, 2, :], reads=[f'{nm[1:].lower()}b{b}'])
        S.dma(DR['MV'][tts, :], mvb[b][:, :], reads=[f'mvb{b}'])

    def st3c(tt):
        b = tt % 2
        tts = slice(tt * 128, (tt + 1) * 128)
        for i, (c0, w) in enumerate(((0, 128), (128, 128))):
            S.op('pe', lambda e, c0=c0, w=w, i=i: e.transpose(psb[6][0:w, 768 + i * 128:768 + (i + 1) * 128], qrb[b][:, c0:c0 + w], ident[:, :]),
                 reads=[f'qrb{b}.o1', f'qrb{b}.o2', f'qrb{b}.pad', 'c_ident'], writes=['ps6'])
        S.op('act', lambda e: e.copy(out=qrT[b][:, :, :], in_=psb[6][:, 768:1024].rearrange("p (c t) -> p c t", t=128)), reads=['ps6'], writes=[f'qrT{b}'])
        S.dma(DR['MQR'][0:128, tts], qrT[b][:, 0, :], reads=[f'qrT{b}'])
        S.dma(DR['MQR'][128:160, tts], qrT[b][0:32, 1, :], reads=[f'qrT{b}'])

    st1a(0)
    if NTT > 1:
        st1a(1)
    st1b(0)
    for it in range(1, NTT + 4):
        if it + 1 < NTT:
            st1a(it + 1)
        if it < NTT:
            st1b(it)
        if 0 <= it - 1 < NTT:
            st2a(it - 1)
        if 0 <= it - 2 < NTT:
            st3a(it - 2)
        if 0 <= it - 1 < NTT:
            st2b(it - 1)
        if 0 <= it - 3 < NTT:
            st3b(it - 3)
        if 0 <= it - 4 < NTT:
            st3c(it - 4)
    S.op('act', lambda e: e.activation(out=graw[:, :, :], in_=graw[:, :, :], func=AF.Sigmoid), reads=[f'graw{tt}' for tt in range(NT)], writes=['gsig'])
    S.dma(DR['GATE'].rearrange("(t p) c -> p t c", p=128), graw[:, :, :], reads=['gsig'])


def attn_stream(g, jobs, scale, sbanks=(0, 1, 2), dummy=None):
    S, ps = g.S, g.ps
    flat = []
    for ji, job in enumerate(jobs):
        for ui, u in enumerate(job['units']):
            flat.append((ji, ui, u))
    prev = None
    first_in_bank = {}
    for ji in range(min(2, len(jobs))):
        if jobs[ji].get('pre'):
            jobs[ji]['pre']()
    bg = None
    LEAD = min(3, len(sbanks) - 1)
    pend = []
    NF = len(flat)
    for i in range(NF + LEAD):
        cur = flat[i] if i < NF else None
        if cur is not None:
            ji, ui, u = cur
            if ui == 0 and ji + 2 < len(jobs) and jobs[ji + 2].get('pre'):
                jobs[ji + 2]['pre']()
            if ui == 0 and jobs[ji].get('need_bg') and bg is not None:
                for _ in bg:
                    pass
                bg = None
            if ui == 3 and jobs[ji].get('bg'):
                bg = jobs[ji]['bg']()
            NB_ = len(sbanks)
            sbk = sbanks[i % NB_]
            pb = g.pbf[i % NB_]
            hm = u['mask'] is not None
            S.op('pe', lambda e, u=u, sbk=sbk, hm=hm: e.matmul(ps[sbk][:, :], lhsT=u['lhsT'], rhs=u['rhs'], start=True, stop=True),
                 reads=u['rk'], writes=[f'ps{sbk}'])
            if hm:
                mj, map_ = u['mask']
                S.op('pe', lambda e, sbk=sbk, mj=mj, map_=map_: e.matmul(ps[sbk][:, mj * 128:(mj + 1) * 128], lhsT=g.C['ident'][:, :], rhs=map_, start=False, stop=True,
                                                                       skip_group_check=True),
                     reads=['c_mtri', 'c_ident'], writes=[f'ps{sbk}'])
            S.op('act', lambda e, sbk=sbk, pb=pb: e.activation(out=pb[:, :], in_=ps[sbk][:, :], func=AF.Exp, scale=scale),
                 reads=[f'ps{sbk}'], writes=[f'pbf{i % NB_}'])
            pend.append((i, cur))
        if dummy is not None and cur is not None:
            S.op('pe', lambda e: e.matmul(ps[dummy][:, 0:256], lhsT=g.C['ident'][:, :], rhs=g.C['mtri'][:, 0:256], start=True, stop=True), reads=['c_ident', 'c_mtri'], writes=[f'ps{dummy}'])
        if bg is not None and i % 7 == 1:
            try:
                next(bg)
            except StopIteration:
                bg = None
        if len(pend) > LEAD or (cur is None and pend):
            pi, (ji, ui, u) = pend.pop(0)
            pb = g.pbf[pi % len(sbanks)]
            ab = 3 + ji % 2
            for j in u['subs']:
                first = (ji, ) not in first_in_bank
                first_in_bank[(ji, )] = True
                S.op('pe', lambda e, pb=pb, u=u, j=j, ab=ab, first=first: e.matmul(ps[ab][:, j * 65:(j + 1) * 65], lhsT=pb[:, j * 128:(j + 1) * 128], rhs=u['vext'],
                                                                                 start=first, stop=False, skip_group_check=True),
                     reads=[f'pbf{pi % len(sbanks)}', u['vk']], writes=[f'ps{ab}'])
            if ui == len(jobs[ji]['units']) - 1:
                jobs[ji]['epi'](ab)
    if bg is not None:
        for _ in bg:
            pass


def phase_mla(g, l):
    S, sb, IN, DR, C, ps = g.S, g.sb, g.IN, g.DR, g.C, g.ps
    S.barrier()
    sb.off = g.persist_mark
    H = 5
    KT = [sb.alloc(f"mKT{h}", [96, S_LEN], BF16) for h in range(H)]
    VX = [sb.alloc(f"mVX{h}", [128, NT, 65], BF16) for h in range(H)]
    QT = [sb.alloc(f"mQT{i}", [96, 512], BF16) for i in range(4)]
    g.pbf = [sb.alloc(f"pbf{i}", [128, 512], BF16) for i in range(6)]
    zt = [sb.alloc(f"mz{i}", [128, 4, 320], F32) for i in range(2)]
    mixt = [sb.alloc(f"mmix{i}", [128, 4, 320], BF16) for i in range(2)]
    rec = [sb.alloc(f"mrec{i}", [128, 4], F32) for i in range(2)]
    dacc = sb.alloc("dacc", [128, 260], F32)
    for h in range(H):
        S.dma(KT[h][0:64, :], DR['MKN'][h * 64:(h + 1) * 64, :], writes=[f'mKT{h}a'])
        S.dma(KT[h][64:96, :], DR['MKPE'][:, :], writes=[f'mKT{h}b'])
        S.op('pool', lambda e, h=h: e.memset(VX[h][:, :, :], 1.0), writes=[f'mVX{h}'])
        S.dma(VX[h][:, :, 0:64], DR['MV'][:, h * 64:(h + 1) * 64].rearrange("(t p) d -> p t d", p=128), writes=[f'mVX{h}'])
    g.dbgout('D_KT', KT[0][:, 0:512], ['mKT0a', 'mKT0b'])
    g.dbgout('D_VX', VX[0][:, 0:4, :], ['mVX0'])
    scale = 96.0 ** -0.5
    jobs = []
    qi = 0
    for c in range(8):
        cs = slice(c * 512, (c + 1) * 512)
        zb = c % 2
        for h in range(H):
            q = qi % 4
            qi += 1

            def pre(h=h, c=c, cs=cs, zb=zb, q=q):
                if h == 0:
                    S.dma(zt[zb][:, :, :], DR['Z'][cs, 704:1024].rearrange("(j p) d -> p j d", p=128), writes=[f'mz{zb}'])
                S.dma(QT[q][0:64, :], DR['MQN'][h * 64:(h + 1) * 64, cs], writes=[f'mQT{q}a'], eng='pool')
                S.dma(QT[q][64:96, :], DR['MQR'][h * 32:(h + 1) * 32, cs], writes=[f'mQT{q}b'], eng='pool')
            units = []
            for kt in range(4 * c + 4):
                i = kt - 4 * c
                units.append(dict(lhsT=KT[h][:, kt * 128:(kt + 1) * 128], rhs=QT[q][:, :], rk=[f'mKT{h}a', f'mKT{h}b', f'mQT{q}a', f'mQT{q}b'],
                                  mask=((i, C['mtri'][:, 0:128]) if i >= 0 else None), mk='c_mtri',
                                  subs=(list(range(i, 4)) if i >= 0 else [0, 1, 2, 3]), vext=VX[h][:, kt, :], vk=f'mVX{h}'))

            def epi(ab, h=h, c=c, zb=zb):
                rb = rec[zb]
                acc = ps[ab][:, 0:260].rearrange("p (j e) -> p j e", e=65)
                if h == 0 and c == 0 and 'D_ACC' in g.dbg:
                    S.op('dve', lambda e: e.tensor_copy(out=dacc[:, :], in_=ps[ab][:, 0:260]), reads=[f'ps{ab}'], writes=['dacc'])
                    g.dbgout('D_ACC', dacc[:, :], ['dacc'])
                S.op('dve', lambda e: e.reciprocal(out=rb[:, :], in_=acc[:, :, 64]), reads=[f'ps{ab}'], writes=[f'mrec{zb}'])
                for j in range(4):
                    S.op('dve', lambda e, j=j: e.scalar_tensor_tensor(out=mixt[zb][:, j, h * 64:(h + 1) * 64], in0=acc[:, j, 0:64], scalar=rb[:, j:j + 1],
                                                                      in1=zt[zb][:, j, h * 64:(h + 1) * 64], op0=ALU.mult, op1=ALU.mult),
                         reads=[f'ps{ab}', f'mrec{zb}', f'mz{zb}'], writes=[f'mmix{zb}.{h}'])
                if h == H - 1:
                    S.dma(DR['MIX'][c * 512:(c + 1) * 512, 704:1024].rearrange("(j p) d -> p j d", p=128), mixt[zb][:, :, :],
                          reads=[f'mmix{zb}.{hh}' for hh in range(H)])
            jobs.append(dict(units=units, epi=epi, pre=pre))
    attn_stream(g, jobs, scale, sbanks=(0, 1, 2, 5, 6, 7))


def phase_sb(g, l):
    S, sb, IN, DR, C, ps = g.S, g.sb, g.IN, g.DR, g.C, g.ps
    S.barrier()
    sb.off = g.persist_mark
    H = 5
    KT = [sb.alloc(f"sKT{h}", [128, S_LEN], BF16) for h in range(H)]
    VV = [sb.alloc(f"sVV{h}", [128, NT, 64], BF16) for h in range(H)]
    QT = [sb.alloc(f"sQT{i}", [128, 512], BF16) for i in range(4)]
    e_sb = [sb.alloc(f"se{i}", [128, 512], F32) for i in range(3)]
    sp_bf = [sb.alloc(f"ssp{i}", [128, 512], BF16) for i in range(2)]
    x_sb = [sb.alloc(f"sx{i}", [128, 512], F32) for i in range(2)]
    a_bf = [sb.alloc(f"sa{i}", [128, 512], BF16) for i in range(2)]
    o_sb = [sb.alloc(f"so{i}", [128, 4, 64], F32) for i in range(2)]
    chist = [sb.alloc(f"sch{i}", [128, 33, 4], F32) for i in range(2)]
    ecar = [sb.alloc(f"sec{i}", [128, 33, 4], F32) for i in range(2)]
    zt = [sb.alloc(f"sz{i}", [128, 4, 320], F32) for i in range(2)]
    mixt = [sb.alloc(f"smix{i}", [128, 4, 320], BF16) for i in range(2)]
    tri = C['tri']
    ones_col = g.ones_bf[:, 0:1]
    for i_ in range(4):
        S.op('pool', lambda e, i_=i_: e.memset(QT[i_][64:128, :], 0.0), writes=[f'sQT{i_}.z'])
    for h in range(H):
        S.op(['pool', 'dve'][h % 2], lambda e, h=h: e.memset(KT[h][64:128, :], 0.0), writes=[f'sKT{h}.z'])
        S.dma(KT[h][0:64, :], DR['SQKT'][320 + h * 64:320 + (h + 1) * 64, :], writes=[f'sKT{h}'])
        S.dma(VV[h][:, :, :], DR['SV'][:, h * 64:(h + 1) * 64].rearrange("(t p) d -> p t d", p=128), writes=[f'sVV{h}'])
    jobs = []
    for c in range(8):
        for h in range(H):
            jobs.append((c, h))
    flat = []
    for ji, (c, h) in enumerate(jobs):
        U = 4 * c + 4
        for u in range(U):
            flat.append((ji, u, U, c, h, 4 * c + 3 - u))

    def pre(ji):
        c, h = jobs[ji]
        cs = slice(c * 512, (c + 1) * 512)
        if h == 0:
            S.dma(zt[c % 2][:, :, :], DR['Z'][cs, 384:704].rearrange("(j p) d -> p j d", p=128), writes=[f'sz{c % 2}'])
        S.dma(QT[ji % 4][0:64, :], DR['SQKT'][h * 64:(h + 1) * 64, cs], writes=[f'sQT{ji % 4}'])

    for ji in range(2):
        pre(ji)
    N = len(flat)

    gtmp = sb.alloc("sgtmp", [128, 4, 64], F32)

    def stA(n):
        ji, u, U, c, h, kt = flat[n]
        if u == 0 and ji + 2 < len(jobs):
            pre(ji + 2)
        zb = n % 2
        i = kt - 4 * c
        S.op('pe', lambda e: e.matmul(ps[zb][:, :], lhsT=KT[h][:, kt * 128:(kt + 1) * 128], rhs=QT[ji % 4][:, :], start=True, stop=True),
             reads=[f'sKT{h}', f'sKT{h}.z', f'sQT{ji % 4}', f'sQT{ji % 4}.z'], writes=[f'ps{zb}'])
        if i >= 0:
            S.op('pe', lambda e: e.matmul(ps[zb][:, i * 128:(i + 1) * 128], lhsT=C['ident'][:, :], rhs=C['mtri'][:, 128:256], start=False, stop=True, skip_group_check=True),
                 reads=['c_ident', 'c_mtri'], writes=[f'ps{zb}'])

    def stB1(n):
        ji, u, U, c, h, kt = flat[n]
        p = ji % 2
        eb, zb = e_sb[n % 3], n % 2
        if u == 0:
            S.op('pool', lambda e: e.memset(o_sb[p][:, :, :], 0.0), writes=[f'so{p}'])
            S.op('pool', lambda e: e.memset(chist[p][:, 0:5, :], 0.0), writes=[f'sch{p}'] + [f'sch{p}.{x}' for x in range(1, 5)])
            S.op('pool', lambda e: e.memset(ecar[p][:, 0, :], 1.0), writes=[f'sec{p}'])
        S.op('act', lambda e: e.activation(out=eb[:, :], in_=ps[zb][:, :], func=AF.Exp, scale=0.125), reads=[f'ps{zb}'], writes=[f'se{n % 3}'])

    def stB2(n):
        eb, spb = e_sb[n % 3], sp_bf[n % 2]
        S.op('act', lambda e: e.activation(out=spb[:, :], in_=eb[:, :], func=AF.Ln, bias=1.0), reads=[f'se{n % 3}'], writes=[f'ssp{n % 2}'])

    def stC(n):
        ji, u, U, c, h, kt = flat[n]
        p = ji % 2
        spb = sp_bf[n % 2]
        cb, kb = 2 + n % 2, 6 + n % 2
        S.op('pe', lambda e: e.matmul(ps[cb][:, :], lhsT=tri[:, :], rhs=spb[:, :], start=True, stop=True), reads=[f'ssp{n % 2}', 'c_tri'], writes=[f'ps{cb}'])
        j0 = max(kt - 4 * c, 0)
        for jj in range(j0, 4):
            S.op('pe', lambda e, jj=jj: e.matmul(ps[kb][:, jj:jj + 1], lhsT=spb[:, jj * 128:(jj + 1) * 128], rhs=ones_col, start=True, stop=True),
                 reads=[f'ssp{n % 2}', 'ones_bf'], writes=[f'ps{kb}'])
        S.op('dve', lambda e: e.tensor_tensor(out=chist[p][:, u + 1, j0:4], in0=chist[p][:, u, j0:4], in1=ps[kb][:, j0:4], op=ALU.add),
             reads=[f'ps{kb}', f'sch{p}', f'sch{p}.{u}'], writes=[f'sch{p}.{u + 1}'])

    def stEcar(n):
        ji, u, U, c, h, kt = flat[n]
        p = ji % 2
        S.op('act', lambda e: e.activation(out=ecar[p][:, u + 1, :], in_=chist[p][:, u + 1, :], func=AF.Exp, scale=-1.0), reads=[f'sch{p}.{u + 1}'], writes=[f'sec{p}.{u + 1}'])

    def stD(n):
        cb = 2 + n % 2
        xb, eb, ab_ = x_sb[n % 2], e_sb[n % 3], a_bf[n % 2]
        S.op('act', lambda e: e.activation(out=xb[:, :], in_=ps[cb][:, :], func=AF.Exp, scale=-1.0), reads=[f'ps{cb}'], writes=[f'sx{n % 2}'])
        S.op('pool', lambda e: e.tensor_tensor(out=ab_[:, 0:320], in0=eb[:, 0:320], in1=xb[:, 0:320], op=ALU.mult), reads=[f'se{n % 3}', f'sx{n % 2}'], writes=[f'sa{n % 2}.a'])
        S.op('dve', lambda e: e.tensor_tensor(out=ab_[:, 320:512], in0=eb[:, 320:512], in1=xb[:, 320:512], op=ALU.mult), reads=[f'se{n % 3}', f'sx{n % 2}'], writes=[f'sa{n % 2}.b'])

    def stF(n):
        ji, u, U, c, h, kt = flat[n]
        p = ji % 2
        i = kt - 4 * c
        act = list(range(max(i, 0), 4))
        ob = 4 + n % 2
        ab_ = a_bf[n % 2]
        ek = [f'sec{p}'] if u == 0 else [f'sec{p}.{u}']
        for jj in act:
            S.op('pe', lambda e, jj=jj: e.matmul(ps[ob][:, jj * 64:(jj + 1) * 64], lhsT=ab_[:, jj * 128:(jj + 1) * 128], rhs=VV[h][:, kt, :], start=True, stop=True),
                 reads=[f'sa{n % 2}.a', f'sa{n % 2}.b', f'sVV{h}'], writes=[f'ps{ob}'])
        if len(act) == 4:
            S.op('dve', lambda e: e.tensor_tensor(out=gtmp[:, :, :], in0=ps[ob][:, 0:256].rearrange("p (j d) -> p j d", d=64),
                                                  in1=fap(ecar[p][:, u, :], [(1, 4), (0, 64)]), op=ALU.mult),
                 reads=[f'ps{ob}'] + ek, writes=['sgtmp'])
            S.op('dve', lambda e: e.tensor_tensor(out=o_sb[p][:, :, :], in0=o_sb[p][:, :, :], in1=gtmp[:, :, :], op=ALU.add),
                 reads=['sgtmp', f'so{p}'], writes=[f'so{p}'])
        else:
            for jj in act:
                S.op('dve', lambda e, jj=jj: e.scalar_tensor_tensor(out=o_sb[p][:, jj, :], in0=ps[ob][:, jj * 64:(jj + 1) * 64], scalar=ecar[p][:, u, jj:jj + 1],
                                                                    in1=o_sb[p][:, jj, :], op0=ALU.mult, op1=ALU.add),
                     reads=[f'ps{ob}', f'so{p}'] + ek, writes=[f'so{p}'])
        if u == U - 1:
            zb = c % 2
            S.op('pool', lambda e: e.tensor_tensor(out=mixt[zb][:, :, h * 64:(h + 1) * 64], in0=o_sb[p][:, :, :], in1=zt[zb][:, :, h * 64:(h + 1) * 64], op=ALU.mult),
                 reads=[f'so{p}', f'sz{zb}'], writes=[f'smix{zb}.{h}'])
            if h == H - 1:
                S.dma(DR['MIX'][c * 512:(c + 1) * 512, 384:704].rearrange("(j p) d -> p j d", p=128), mixt[zb][:, :, :],
                      reads=[f'smix{zb}.{hh}' for hh in range(H)])

    stA(0)
    if N > 1:
        stA(1)
    for it in range(N + 1):
        if it < N:
            stB1(it)
        if it >= 1:
            stD(it - 1)
        if it < N:
            stB2(it)
        if it >= 1:
            stEcar(it - 1)
        if it < N:
            stC(it)
        if it + 2 < N:
            stA(it + 2)
        if it >= 1:
            stF(it - 1)


def phase_nsa(g, l):
    S, sb, IN, DR, C, ps, psb, nc = g.S, g.sb, g.IN, g.DR, g.C, g.ps, g.psb, g.nc
    S.barrier()
    sb.off = g.persist_mark
    ident = C['ident']
    w1 = [sb.alloc(f"nw1{t}", [64, 32, 128], BF16) for t in range(2)]
    w2 = [sb.alloc(f"nw2{t}", [128, 64], BF16) for t in range(2)]
    posT = [sb.alloc(f"npos{t}", [64, 32], BF16) for t in range(2)]
    KC = [sb.alloc(f"nKC{gg}", [64, 256], BF16) for gg in range(2)]
    VCc = [sb.alloc(f"nVC{gg}", [128, 2, 64], BF16) for gg in range(2)]
    bias = sb.alloc("nbias", [128, 2], F32)
    mark = sb.off
    stg = sb.alloc("nstg", [64, 32, 128], F32)
    stg2 = sb.alloc("nstg2", [128, 128], F32)
    src = [[sb.alloc(f"nsrc{t}{gg}", [64, S_LEN], BF16) for gg in range(2)] for t in range(2)]
    hs = [sb.alloc(f"nhs{i}", [128, 256], BF16) for i in range(2)]
    for i_ in range(2):
        S.op('pool', lambda e, i_=i_: e.memset(hs[i_][:, :], 0.0), writes=[f'nhs{i_}'])
    for t, (n1, n2, npos) in enumerate((('ck_w1', 'ck_w2', 'pos_kT'), ('cv_w1', 'cv_w2', 'pos_vT'))):
        S.dma(stg[:, :, :], IN[n1][l].rearrange("(a d) h -> d a h", d=64), writes=['nstg'])
        S.op('dve', lambda e, t=t: e.tensor_copy(out=w1[t][:, :, :], in_=stg[:, :, :]), reads=['nstg'], writes=[f'nw1{t}'])
        S.dma(stg2[:, 0:64], IN[n2][l], writes=['nstg2'])
        S.dma(stg2[0:64, 64:96], IN[npos][l], writes=['nstg2'])
        S.op('dve', lambda e, t=t: e.tensor_copy(out=w2[t][:, :], in_=stg2[:, 0:64]), reads=['nstg2'], writes=[f'nw2{t}'])
        S.op('dve', lambda e, t=t: e.tensor_copy(out=posT[t][:, :], in_=stg2[0:64, 64:96]), reads=['nstg2'], writes=[f'npos{t}'])
        for gg in range(2):
            if t == 0:
                S.dma(src[t][gg][:, :], DR['R64T'][(6 + gg) * 64:(7 + gg) * 64, :], writes=[f'nsrc{t}{gg}'])
            else:
                S.dma(src[t][gg][:, :], DR['VCT'][gg * 64:(gg + 1) * 64, :], writes=[f'nsrc{t}{gg}'])
    for t in range(2):
        for a in range(32):
            S.op('pe', lambda e, t=t, a=a: e.matmul(ps[4][:, t:t + 1], lhsT=w1[t][:, a, :], rhs=posT[t][:, a:a + 1], start=(a == 0), stop=(a == 31)),
                 reads=[f'nw1{t}', f'npos{t}'], writes=['ps4'])
    S.op('dve', lambda e: e.tensor_copy(out=bias[:, :], in_=ps[4][:, 0:2]), reads=['ps4'], writes=['nbias'])
    for t in range(2):
        for gg in range(2):
            bk = t * 2 + gg
            hb = hs[bk % 2]
            for a in range(32):
                rhs = fap(src[t][gg][:, a:a + 1], [(16, 255)])
                S.op('pe', lambda e, t=t, a=a, bk=bk, rhs=rhs: e.matmul(ps[bk][:, 0:255], lhsT=w1[t][:, a, :], rhs=rhs, start=(a == 0), stop=(a == 31)),
                     reads=[f'nw1{t}', f'nsrc{t}{gg}'], writes=[f'ps{bk}'])
            S.op('act', lambda e, t=t, bk=bk, hb=hb: e.activation(out=hb[:, 0:255], in_=ps[bk][:, 0:255], func=AF.Silu, bias=bias[:, t:t + 1]),
                 reads=[f'ps{bk}', 'nbias'], writes=[f'nhs{bk % 2}'])
            if t == 0:
                S.op('pe', lambda e, t=t, hb=hb: e.matmul(ps[5][0:64, 0:255], lhsT=w2[t][:, :], rhs=hb[:, 0:255], start=True, stop=True),
                     reads=[f'nhs{bk % 2}', f'nw2{t}'], writes=['ps5'])
                S.op('dve', lambda e, gg=gg: e.tensor_copy(out=KC[gg][:, 0:255], in_=ps[5][0:64, 0:255]), reads=['ps5'], writes=[f'nKC{gg}'])
            else:
                S.op('pe', lambda e, t=t, hb=hb: e.matmul(ps[5][:, 256:320], lhsT=hb[:, 0:128], rhs=w2[t][:, :], start=True, stop=True),
                     reads=[f'nhs{bk % 2}', f'nw2{t}'], writes=['ps5'])
                S.op('pe', lambda e, t=t, hb=hb: e.matmul(ps[5][:, 320:384], lhsT=hb[:, 128:256], rhs=w2[t][:, :], start=True, stop=True, skip_group_check=True),
                     reads=[f'nhs{bk % 2}', f'nw2{t}'], writes=['ps5'])
                S.op('dve', lambda e, gg=gg: e.tensor_copy(out=VCc[gg][:, :, :], in_=ps[5][:, 256:384].rearrange("p (a d) -> p a d", d=64)), reads=['ps5'], writes=[f'nVC{gg}'])
    S.barrier()
    sb.off = mark
    KSa = sb.alloc("nKSa", [128, S_LEN], BF16)
    KWT = sb.alloc("nKWT", [128, S_LEN], BF16)
    VSx = sb.alloc("nVSx", [128, NT, 65], BF16)
    VWx = sb.alloc("nVWx", [128, NT, 65], BF16)
    ohst = sb.alloc("nohst", [128, S_LEN], F32)
    Qa = [[sb.alloc(f"nQa{p}{hh}", [128, 512], BF16) for hh in range(3)] for p in range(2)]
    g.pbf = [sb.alloc(f"pbf{i}", [128, 512], BF16) for i in range(3)]
    gt = [sb.alloc(f"ngt{p}", [128, 4, 18], F32) for p in range(2)]
    zt = [sb.alloc(f"nzt{p}", [128, 4, 192], F32) for p in range(2)]
    nacc = [sb.alloc(f"nacc{p}", [128, 4, 3, 64], F32) for p in range(2)]
    mixt = [sb.alloc(f"nmix{p}", [128, 4, 192], BF16) for p in range(2)]
    ec = [sb.alloc(f"nec{i}", [128, 256], F32) for i in range(2)]
    pm = [sb.alloc(f"npm{i}", [128, 256], F32) for i in range(2)]
    pmb = [sb.alloc(f"npmb{i}", [128, 256], BF16) for i in range(2)]
    pT = [sb.alloc(f"npT{i}", [128, 2, 128], BF16) for i in range(2)]
    P3 = sb.alloc("nP3", [128, 260], F32)
    imp = sb.alloc("nimp", [128, 64], F32)
    imp2 = sb.alloc("nimp2", [128, 64], F32)
    rep = [sb.alloc(f"nrep{i}", [128, 64], F32) for i in range(2)]
    m8 = sb.alloc("nm8", [128, 16], F32)
    selb = sb.alloc("nselb", [128, 128], BF16)
    st = sb.alloc("nst", [128, 64], F32)
    rec = sb.alloc("nrec", [128, 8], F32)
    S.dma(ohst[64:128, :], IN['onehot'][64:128, :], writes=['nohst'])
    S.op('pool', lambda e: e.tensor_copy(out=KSa[64:128, :], in_=ohst[64:128, :]), reads=['nohst'], writes=['nKSa.oh'])
    S.op('dve', lambda e: e.memset(selb[:, :], 0.0), writes=['nselb'])
    S.op('dve', lambda e: e.memset(KWT[64:128, :], 0.0), writes=['nKWT.z'])
    for p_ in range(2):
        for hh_ in range(3):
            S.op('pool', lambda e, p_=p_, hh_=hh_: e.memset(Qa[p_][hh_][64:128, :], 0.0), writes=[f'nQa{p_}{hh_}.s{j_}' for j_ in range(4)])
    for i_ in range(2):
        S.op('pool', lambda e, i_=i_: e.memset(pmb[i_][:, :], 0.0), writes=[f'npmb{i_}'])
    stc = [0]

    def cmp_select(gg, c, p):
        cs = slice(c * 512, (c + 1) * 512)
        S.dma(gt[p][:, :, :], DR['GATE'][cs, :].rearrange("(j p) c -> p j c", p=128), writes=[f'ngt{p}'])
        S.dma(zt[p][:, :, :], DR['Z'][cs, gg * 192:(gg + 1) * 192].rearrange("(j p) d -> p j d", p=128), writes=[f'nzt{p}'])
        for hh in range(3):
            S.dma(Qa[p][hh][0:64, :], DR['R64T'][(gg * 3 + hh) * 64:(gg * 3 + hh + 1) * 64, cs], writes=[f'nQa{p}{hh}.q'])
        for _ in range(3):
            yield
        items = [(j, hh) for j in range(4) for hh in range(3)]
        NI = len(items)

        def geo(n):
            j, hh = items[n]
            i = 4 * c + j
            ncmp = min(255, 8 * i + 8)
            tiles = [(0, min(128, ncmp))] + ([(128, ncmp - 128)] if ncmp > 128 else [])
            return j, hh, i, ncmp, tiles, n % 2, (n % 8) * 4

        def s1(n):
            j, hh, i, ncmp, tiles, k, s0 = geo(n)
            gcol = (gg * 3 + hh) * 3
            if hh == 0:
                S.op('pool', lambda e: e.memset(P3[:, :], 0.0), writes=['nP3'])
            S.op('pe', lambda e: e.matmul(ps[5][:, 0:ncmp], lhsT=Qa[p][hh][0:64, j * 128:(j + 1) * 128], rhs=KC[gg][:, 0:ncmp], start=True, stop=True),
                 reads=[f'nQa{p}{hh}.q', f'nKC{gg}'], writes=['ps5'])
            S.op('act', lambda e: e.activation(out=ec[k][:, 0:ncmp], in_=ps[5][:, 0:ncmp], func=AF.Exp, scale=0.125), reads=['ps5'], writes=[f'nec{k}'])
            S.op('dve', lambda e: e.memset(st[:, s0:s0 + 1], 0.0), writes=[f'nst{s0}'])
            S.op('dve', lambda e: e.scalar_tensor_tensor(out=pm[k][:, 0:ncmp], in0=ec[k][:, 0:ncmp], scalar=1.0, in1=C['mbig'][:, 248 - 8 * i:248 - 8 * i + ncmp],
                                                         op0=ALU.mult, op1=ALU.mult, accum_out=st[:, s0:s0 + 1]),
                 reads=[f'nec{k}', 'c_mbig', f'nst{s0}'], writes=[f'npm{k}', f'nst{s0}'])
            S.op('pool', lambda e: e.tensor_copy(out=pmb[k][:, 0:ncmp], in_=pm[k][:, 0:ncmp]), reads=[f'npm{k}'], writes=[f'npmb{k}'])
            S.op('dve', lambda e: e.tensor_scalar(out=st[:, s0 + 1:s0 + 2], in0=st[:, s0:s0 + 1], scalar1=1e-30, scalar2=None, op0=ALU.max), reads=[f'nst{s0}'], writes=[f'nst{s0}'])
            S.op('dve', lambda e: e.reciprocal(out=st[:, s0 + 1:s0 + 2], in_=st[:, s0 + 1:s0 + 2]), reads=[f'nst{s0}'], writes=[f'nst{s0}'])
            S.op('dve', lambda e: e.scalar_tensor_tensor(out=P3[:, 1:1 + ncmp], in0=pm[k][:, 0:ncmp], scalar=st[:, s0 + 1:s0 + 2], in1=P3[:, 1:1 + ncmp],
                                                         op0=ALU.mult, op1=ALU.add), reads=[f'npm{k}', f'nst{s0}', 'nP3'], writes=['nP3'])
            S.op('dve', lambda e: e.tensor_tensor(out=st[:, s0 + 2:s0 + 3], in0=st[:, s0 + 1:s0 + 2], in1=gt[p][:, j, gcol:gcol + 1], op=ALU.mult),
                 reads=[f'nst{s0}', f'ngt{p}'], writes=[f'nst{s0}'])
            if hh == 2:
                S.op('dve', lambda e: e.tensor_reduce(out=imp[:, :], in_=P3[:, 0:256].rearrange("p (b m) -> p b m", m=4), axis=AX.X, op=ALU.add), reads=['nP3'], writes=['nimp'])
                S.op('dve', lambda e: e.tensor_tensor(out=imp[:, :], in0=imp[:, :], in1=fap(P3[:, 4:5], [(4, 64)]), op=ALU.add), reads=['nP3', 'nimp'], writes=['nimp'])
                S.op('dve', lambda e: e.tensor_tensor(out=imp2[:, :], in0=imp[:, :], in1=C['selkeep'][:, 62 - 2 * i:126 - 2 * i], op=ALU.mult), reads=['nimp', 'c_selkeep'], writes=['nimp2'])
                S.op('dve', lambda e: e.tensor_tensor(out=imp2[:, :], in0=imp2[:, :], in1=C['seladd'][:, 62 - 2 * i:126 - 2 * i], op=ALU.add), reads=['nimp2', 'c_seladd'], writes=['nimp2'])
                S.op('dve', lambda e: e.memset(imp2[:, 0:1], 1.2e9), reads=['nimp2'], writes=['nimp2'])
                S.op('dve', lambda e: e.max(out=m8[:, 0:8], in_=imp2[:, :]), reads=['nimp2'], writes=['nm8'])
                S.op('dve', lambda e: e.match_replace(out=rep[0][:, :], in_to_replace=m8[:, 0:8], in_values=imp2[:, :], imm_value=-3e9), reads=['nimp2', 'nm8'], writes=['nrep0'])
                S.op('dve', lambda e: e.max(out=m8[:, 8:16], in_=rep[0][:, :]), reads=['nrep0'], writes=['nm8'])
                S.op('dve', lambda e: e.match_replace(out=rep[1][:, :], in_to_replace=m8[:, 8:16], in_values=rep[0][:, :], imm_value=-3e9), reads=['nrep0', 'nm8'], writes=['nrep1'])
                S.op('dve', lambda e: e.tensor_scalar(out=selb[:, 64:128], in0=rep[1][:, :], scalar1=-2e9, scalar2=-30000.0, op0=ALU.is_gt, op1=ALU.mult), reads=['nrep1'], writes=['nselb'])

        def s2(n):
            j, hh, i, ncmp, tiles, k, s0 = geo(n)
            for ti, (c0, w) in enumerate(tiles):
                S.op('pe', lambda e, ti=ti, c0=c0: e.transpose(psb[6][:, ti * 128:(ti + 1) * 128], pmb[k][:, c0:c0 + 128], ident[:, :]),
                     reads=[f'npmb{k}', 'c_ident'], writes=['ps6'])
            nt_ = len(tiles)
            S.op('act', lambda e: e.copy(out=pT[k][:, 0:nt_, :], in_=psb[6][:, 0:128 * nt_].rearrange("p (a q) -> p a q", q=128)), reads=['ps6'], writes=[f'npT{k}'])

        def s3(n):
            j, hh, i, ncmp, tiles, k, s0 = geo(n)
            for ti, (c0, w) in enumerate(tiles):
                S.op('pe', lambda e, ti=ti, w=w: e.matmul(ps[7][:, hh * 64:(hh + 1) * 64], lhsT=pT[k][0:w, ti, :], rhs=VCc[gg][0:w, ti, :],
                                                        start=(ti == 0), stop=(ti == len(tiles) - 1)),
                     reads=[f'npT{k}', f'nVC{gg}'], writes=['ps7'])
            S.op('dve', lambda e: e.tensor_scalar(out=nacc[p][:, j, hh, :], in0=ps[7][:, hh * 64:(hh + 1) * 64], scalar1=st[:, s0 + 2:s0 + 3], scalar2=None, op0=ALU.mult),
                 reads=['ps7', f'nst{s0}'], writes=[f'nacc{p}.{hh}'])

        def sel_b(j):
            S.op('pe', lambda e: e.transpose(psb[6][:, 256:384], selb[:, :], ident[:, :]), reads=['nselb', 'c_ident'], writes=['ps6'])

        def sel_c(j):
            for hh in range(3):
                if hh == 1:
                    S.op('dve', lambda e, hh=hh: e.tensor_copy(out=Qa[p][hh][64:128, j * 128:(j + 1) * 128], in_=psb[6][64:128, 256:384]), reads=['ps6'], writes=[f'nQa{p}{hh}.s{j}'])
                else:
                    S.op('act', lambda e, hh=hh: e.copy(out=Qa[p][hh][64:128, j * 128:(j + 1) * 128], in_=psb[6][64:128, 256:384]), reads=['ps6'], writes=[f'nQa{p}{hh}.s{j}'])

        for m in range(NI + 3):
            if m < NI:
                s1(m)
            if 0 <= m - 1 < NI:
                s2(m - 1)
            if 0 <= m - 2 < NI:
                s3(m - 2)
            if m >= 3 and (m - 3) % 3 == 0 and (m - 3) // 3 < 4:
                sel_b((m - 3) // 3)
                sel_c((m - 3) // 3)
            yield

    for gg in range(2):
        S.dma(KSa[0:64, :], DR['R64T'][(8 + gg) * 64:(9 + gg) * 64, :], writes=['nKSa.k'])
        S.dma(KWT[0:64, :], DR['R64T'][(10 + gg) * 64:(11 + gg) * 64, :], writes=['nKWT'])
        S.op('pool', lambda e: e.memset(VSx[:, :, :], 1.0), writes=['nVSx'])
        S.op('pool', lambda e: e.memset(VWx[:, :, :], 1.0), writes=['nVWx'])
        S.dma(VSx[:, :, 0:64], DR['VSW'][:, gg * 64:(gg + 1) * 64].rearrange("(t p) d -> p t d", p=128), writes=['nVSx'])
        S.dma(VWx[:, :, 0:64], DR['VSW'][:, 128 + gg * 64:128 + (gg + 1) * 64].rearrange("(t p) d -> p t d", p=128), writes=['nVWx'])
        jobs = []
        for c in range(8):
            p = c % 2
            for br in (1, 2):
                for hh in range(3):
                    qk = [f'nQa{p}{hh}.q'] + ([f'nQa{p}{hh}.s{j}' for j in range(4)] if br == 1 else [])
                    units = []
                    if br == 1:
                        for kt in range(4 * c + 4):
                            i = kt - 4 * c
                            units.append(dict(lhsT=KSa[:, kt * 128:(kt + 1) * 128], rhs=Qa[p][hh][:, :], rk=['nKSa.k', 'nKSa.oh'] + qk,
                                              mask=((i, C['mtri'][:, 0:128]) if i >= 0 else None), mk='c_mtri',
                                              subs=(list(range(i, 4)) if i >= 0 else [0, 1, 2, 3]), vext=VSx[:, kt, :], vk='nVSx'))
                    else:
                        for kt in range(max(0, 4 * c - 4), 4 * c + 4):
                            o = kt - 4 * c
                            subs = list(range(0, o + 5)) if o < 0 else list(range(o, 4))
                            units.append(dict(lhsT=KWT[:, kt * 128:(kt + 1) * 128], rhs=Qa[p][hh][:, :], rk=['nKWT', 'nKWT.z'] + qk + [f'nQa{p}{hh}.s{j_}' for j_ in range(4)],
                                              mask=((o + 4, C['mtri'][:, 256:384]) if o < 0 else (o, C['mtri'][:, 0:128])), mk='c_mtri', subs=subs, vext=VWx[:, kt, :], vk='nVWx'))

                    def epi(ab, gg=gg, c=c, p=p, br=br, hh=hh):
                        acc = ps[ab][:, 0:260].rearrange("p (j e) -> p j e", e=65)
                        gcol = (gg * 3 + hh) * 3 + br
                        S.op('dve', lambda e: e.reciprocal(out=rec[:, 0:4], in_=acc[:, :, 64]), reads=[f'ps{ab}'], writes=['nrec'])
                        S.op('dve', lambda e: e.tensor_tensor(out=rec[:, 4:8], in0=rec[:, 0:4], in1=gt[p][:, :, gcol], op=ALU.mult), reads=['nrec', f'ngt{p}'], writes=['nrec'])
                        for j in range(4):
                            S.op('dve', lambda e, j=j: e.scalar_tensor_tensor(out=nacc[p][:, j, hh, :], in0=acc[:, j, 0:64], scalar=rec[:, 4 + j:5 + j], in1=nacc[p][:, j, hh, :],
                                                                              op0=ALU.mult, op1=ALU.add), reads=[f'ps{ab}', 'nrec', f'nacc{p}.{hh}'], writes=[f'nacc{p}.{hh}'])
                        if br == 2:
                            S.op('pool', lambda e: e.tensor_tensor(out=mixt[p][:, :, hh * 64:(hh + 1) * 64], in0=nacc[p][:, :, hh, :], in1=zt[p][:, :, hh * 64:(hh + 1) * 64], op=ALU.mult),
                                 reads=[f'nacc{p}.{hh}', f'nzt{p}'], writes=[f'nmix{p}.{hh}'])
                            if hh == 2:
                                S.dma(DR['MIX'][c * 512:(c + 1) * 512, gg * 192:(gg + 1) * 192].rearrange("(j p) d -> p j d", p=128), mixt[p][:, :, :],
                                      reads=[f'nmix{p}.{h2}' for h2 in range(3)])
                    job = dict(units=units, epi=epi)
                    if br == 1 and hh == 0:
                        if c == 0:
                            job['pre'] = (lambda gg=gg: [None for _ in cmp_select(gg, 0, 0)])
                        else:
                            job['need_bg'] = True
                        if c + 1 < 8:
                            job['bg'] = (lambda gg=gg, c=c: cmp_select(gg, c + 1, (c + 1) % 2))
                    jobs.append(job)
        attn_stream(g, jobs, 0.125)


def phase_outproj(g, l, last):
    S, sb, IN, DR, C, ps, psb = g.S, g.sb, g.IN, g.DR, g.C, g.ps, g.psb
    S.barrier()
    sb.off = g.persist_mark
    WO = sb.alloc("WO", [128, 8, D], BF16)
    wst = [sb.alloc(f"wst{i}", [128, D], F32) for i in range(2)]
    mx = [sb.alloc(f"omx{i}", [128, D], BF16) for i in range(2)]
    mT = [sb.alloc(f"omT{i}", [128, 8, 128], BF16) for i in range(2)]
    xt = [sb.alloc(f"oxt{i}", [128, D], F32) for i in range(2)]
    yt = [sb.alloc(f"oyt{i}", [128, D], F32) for i in range(2)]
    junk = sb.alloc("ojunk", [128, D], BF16)
    ss = sb.alloc("oss", [128, NT], F32)
    rs = sb.alloc("ors", [128, NT], F32)
    ident = C['ident']
    for k in range(8):
        b = k % 2
        S.dma(wst[b][:, :], IN['w_out'][l, k * 128:(k + 1) * 128, :], writes=[f'wst{b}'])
        S.op(['dve', 'pool'][k % 2], lambda e, b=b, k=k: e.tensor_copy(out=WO[:, k, :], in_=wst[b][:, :]), reads=[f'wst{b}'], writes=[f'WO{k}'])
    S.op('dve', lambda e: e.memset(ss[:, :], 0.0), writes=['oss'])
    xsrc = IN['x'] if l == 0 else DR['X1']
    mhalf = sb.alloc("omhalf", [128, 2], F32)
    S.op('pool', lambda e: e.memset(mhalf[:, :], -0.5), writes=['omhalf'])

    def o1(tt):
        b = tt % 2
        tts = slice(tt * 128, (tt + 1) * 128)
        S.dma(mx[b][:, :], DR['MIX'][tts, :], writes=[f'omx{b}'], eng='pool')
        S.dma(xt[b][:, :], xsrc[tts, :], writes=[f'oxt{b}'], eng='pool')
        for c in range(8):
            S.op('pe', lambda e, c=c: e.transpose(psb[7 - b][:, c * 128:(c + 1) * 128], mx[b][:, c * 128:(c + 1) * 128], ident[:, :]),
                 reads=[f'omx{b}', 'c_ident'], writes=[f'ps{7 - b}'])
        S.op('act', lambda e: e.copy(out=mT[b][:, :, :], in_=psb[7 - b][:, :].rearrange("p (c t) -> p c t", t=128)), reads=[f'ps{7 - b}'], writes=[f'omT{b}'])

    def o2(tt):
        b = tt % 2
        tts = slice(tt * 128, (tt + 1) * 128)
        for hf in range(2):
            bk = 2 * b + hf
            for k in range(8):
                S.op('pe', lambda e, k=k, hf=hf, bk=bk: e.matmul(ps[bk][:, :], lhsT=mT[b][:, k, :], rhs=WO[:, k, hf * 512:(hf + 1) * 512], start=(k == 0), stop=(k == 7)),
                     reads=[f'omT{b}', f'WO{k}'], writes=[f'ps{bk}'])
            S.op('dve', lambda e, hf=hf, bk=bk: e.tensor_tensor(out=yt[b][:, hf * 512:(hf + 1) * 512], in0=ps[bk][:, :], in1=g.gate_b[:, hf * 512:(hf + 1) * 512], op=ALU.mult),
                 reads=[f'ps{bk}', 'gate_b'], writes=[f'oyt{b}.{hf}'])
        S.op('pool', lambda e: e.tensor_tensor(out=yt[b][:, :], in0=yt[b][:, :], in1=xt[b][:, :], op=ALU.add),
             reads=[f'oyt{b}.0', f'oyt{b}.1', f'oxt{b}'], writes=[f'oyt{b}.0', f'oyt{b}.1'])
        if not last:
            S.dma(DR['X1'][tts, :], yt[b][:, :], reads=[f'oyt{b}.0', f'oyt{b}.1'])
        else:
            S.op('act', lambda e: e.activation(out=junk[:, :], in_=yt[b][:, :], func=AF.Square, accum_out=ss[:, tt:tt + 1]),
                 reads=[f'oyt{b}.0', f'oyt{b}.1', 'oss'], writes=['ojunk', f'oss{tt}'])
            S.op('dve', lambda e: e.tensor_scalar(out=rs[:, tt:tt + 1], in0=ss[:, tt:tt + 1], scalar1=1.0 / D, scalar2=EPS, op0=ALU.mult, op1=ALU.add),
                 reads=[f'oss{tt}'], writes=[f'ors{tt}'])
            S.op('pool', lambda e: e.tensor_tensor(out=rs[:, tt:tt + 1], in0=rs[:, tt:tt + 1], in1=mhalf[:, 0:1], op=ALU.pow), reads=[f'ors{tt}', 'omhalf'], writes=[f'ors{tt}'])
            S.op('dve', lambda e: e.scalar_tensor_tensor(out=yt[b][:, :], in0=yt[b][:, :], scalar=rs[:, tt:tt + 1], in1=g.fnorm_b[:, :], op0=ALU.mult, op1=ALU.mult),
                 reads=[f'oyt{b}.0', f'oyt{b}.1', f'ors{tt}', 'fnorm_b'], writes=[f'oyt{b}.0', f'oyt{b}.1'])
            S.dma(g.out[tts, :], yt[b][:, :], reads=[f'oyt{b}.0', f'oyt{b}.1'], is_output=True)

    o1(0)
    for it in range(NT):
        if it + 1 < NT:
            o1(it + 1)
        o2(it)


class K:
    pass


def build(dbg=(), phases=('inproj', 'mla', 'sb', 'nsa', 'outproj'), nlayers=DEPTH):
    nc = bass.Bass("TRN2", target_bir_lowering=False)
    S = Sched(nc)
    sb = SB(nc)
    g = K()
    g.nc, g.S, g.sb = nc, S, sb
    IN = {}
    for n, (shp, dt) in IN_SHAPES.items():
        IN[n] = nc.dram_tensor(n, shp, dt, kind="ExternalInput").ap()
    for n, shp in CONST_SHAPES.items():
        IN[n] = nc.dram_tensor(n, shp, F32, kind="ExternalInput").ap()
    g.IN = IN
    out = nc.dram_tensor("out", [S_LEN, D], F32, kind="ExternalOutput").ap()
    g.out = out

    def scratch(name, shape, dt):
        kind = "ExternalOutput" if name in dbg else "Internal"
        return nc.dram_tensor(name, shape, dt, kind=kind).ap()
    DR = {}
    for name, shape, dt in [
        ('R64T', [768, S_LEN], BF16), ('VCT', [128, S_LEN], BF16), ('SQKT', [640, S_LEN], BF16),
        ('MKPE', [32, S_LEN], BF16), ('VSW', [S_LEN, 256], BF16), ('SV', [S_LEN, 320], BF16),
        ('Z', [S_LEN, 1024], F32), ('GATE', [S_LEN, 18], F32), ('MV', [S_LEN, 320], BF16),
        ('MQN', [320, S_LEN], BF16), ('MQR', [160, S_LEN], BF16), ('MKN', [320, S_LEN], BF16),
        ('MIX', [S_LEN, 1024], BF16), ('X1', [S_LEN, D], F32), ('GROW', [1, D], F32),
    ]:
        DR[name] = scratch(name, shape, dt)
    g.DR = DR
    g.dbg = dbg

    def dbgout(name, ap, keys, dt=None):
        if name not in dbg:
            return
        t = nc.dram_tensor(name, list(ap.shape), dt or ap.dtype, kind='ExternalOutput').ap()
        S.dma(t, ap, reads=keys, is_output=True)
    g.dbgout = dbgout

    g.ps = [nc.alloc_psum_tensor(f"bank{i}", [128, 512], F32) for i in range(8)]
    g.psb = [t.bitcast(BF16) for t in g.ps]

    C = {}
    stage = nc.alloc_sbuf_tensor_at("cstage", [128, 4096], F32, offset=200 * 1024)

    def load_const(name, dt, shape2d):
        t = sb.alloc("c_" + name, [128, shape2d], dt)
        src = IN[name]
        if len(src.shape) == 3:
            src = src.rearrange("p a b -> p (a b)")
        if dt == F32:
            S.dma(t[:, :], src, writes=['c_' + name])
        else:
            S.dma(stage[:, 0:shape2d], src, writes=['cstage'])
            S.op('dve', lambda e: e.tensor_copy(out=t[:, :], in_=stage[:, 0:shape2d]), reads=['cstage'], writes=['c_' + name])
        C[name] = t
    load_const('ident', BF16, 128)
    load_const('m_causal', BF16, 2048)
    load_const('m_strict', BF16, 2048)
    load_const('m_win', BF16, 4096)
    load_const('tri', BF16, 128)
    load_const('mtri', BF16, 384)
    load_const('inv', F32, 48)
    load_const('mbig', F32, 503)
    load_const('selkeep', F32, 126)
    load_const('seladd', F32, 126)
    g.C = C
    ones_bf = sb.alloc("ones_bf", [128, 128], BF16)
    S.op('dve', lambda e: e.memset(ones_bf[:, :], 1.0), writes=['ones_bf'])
    ones_f = sb.alloc("ones_f", [128, 128], F32)
    S.op('dve', lambda e: e.memset(ones_f[:, :], 1.0), writes=['ones_f'])
    g.ones_bf, g.ones_f = ones_bf, ones_f

    pos_i = sb.alloc("pos_i", [128, NT], I32)
    pos_f = sb.alloc("pos_f", [128, NT], F32)
    S.dma(pos_i[:, :], IN['pos_pt'], writes=['pos_i'])
    S.op('dve', lambda e: e.tensor_copy(out=pos_f[:, :], in_=pos_i[:, :]), reads=['pos_i'], writes=['pos_f'])
    tabs = {}
    for nm, i0, n in (('64', 0, 32), ('32', 32, 16)):
        ang = nc.alloc_sbuf_tensor_at("ang" + nm, [128, NT, n], F32, offset=(176 if nm == "64" else 192) * 1024)
        tmp = nc.alloc_sbuf_tensor_at("angt" + nm, [128, NT, n], F32, offset=(182 if nm == "64" else 195) * 1024)
        cs = sb.alloc("cos" + nm, [128, NT, n], F32)
        sn = sb.alloc("sin" + nm, [128, NT, n], F32)
        a_pos = fap(pos_f[:, :], [(1, NT), (0, n)])
        a_inv = fap(C['inv'][:, i0:i0 + n], [(0, NT), (1, n)])
        S.op('dve', lambda e, ang=ang, a_pos=a_pos, a_inv=a_inv: e.tensor_tensor(out=ang[:, :, :], in0=a_pos, in1=a_inv, op=ALU.mult),
             reads=['pos_f', 'c_inv'], writes=['ang' + nm])
        ti = nc.alloc_sbuf_tensor_at("angi" + nm, [128, NT, n], I32, offset=(188 if nm == "64" else 198) * 1024)
        for which, shift, dst in (('s', 0.0, sn), ('c', 0.5 * PI, cs)):
            S.op('dve', lambda e, ang=ang, tmp=tmp, shift=shift: e.tensor_scalar(out=tmp[:, :, :], in0=ang[:, :, :], scalar1=shift, scalar2=1.0 / (2 * PI), op0=ALU.add, op1=ALU.mult),
                 reads=['ang' + nm, 'sin' + nm, 'cos' + nm], writes=['angt' + nm])
            S.op('dve', lambda e, tmp=tmp, ti=ti: e.tensor_copy(out=ti[:, :, :], in_=tmp[:, :, :]), reads=['angt' + nm], writes=['angi' + nm])
            S.op('dve', lambda e, tmp=tmp, ti=ti: e.tensor_copy(out=tmp[:, :, :], in_=ti[:, :, :]), reads=['angi' + nm], writes=['angt' + nm])
            S.op('dve', lambda e, ang=ang, tmp=tmp: e.scalar_tensor_tensor(out=tmp[:, :, :], in0=tmp[:, :, :], scalar=-2 * PI, in1=ang[:, :, :], op0=ALU.mult, op1=ALU.add),
                 reads=['angt' + nm, 'ang' + nm], writes=['angt' + nm])
            S.op('dve', lambda e, tmp=tmp, shift=shift: e.tensor_scalar(out=tmp[:, :, :], in0=tmp[:, :, :], scalar1=shift, scalar2=None, op0=ALU.add),
                 reads=['angt' + nm], writes=['angt' + nm])
            S.op('dve', lambda e, tmp=tmp, ti=ti: e.tensor_scalar(out=ti[:, :, :].bitcast(F32), in0=tmp[:, :, :], scalar1=PI, scalar2=2 * PI, op0=ALU.is_gt, op1=ALU.mult),
                 reads=['angt' + nm], writes=['angi' + nm])
            S.op('dve', lambda e, tmp=tmp, ti=ti: e.tensor_tensor(out=tmp[:, :, :], in0=tmp[:, :, :], in1=ti[:, :, :].bitcast(F32), op=ALU.subtract),
                 reads=['angt' + nm, 'angi' + nm], writes=['angt' + nm])
            S.op('act', lambda e, tmp=tmp, dst=dst: e.activation(out=dst[:, :, :], in_=tmp[:, :, :], func=AF.Sin),
                 reads=['angt' + nm], writes=[('sin' if which == 's' else 'cos') + nm])
        tabs[nm] = (cs, sn)
    g.tabs = tabs
    g.gate_b = sb.alloc("gate_b", [128, D], F32)
    g.modA = sb.alloc("modA", [128, 8], F32)
    g.modS = sb.alloc("modS", [128, 8], F32)
    g.fnorm_b = sb.alloc("fnorm_b", [128, D], F32)
    S.dma(g.fnorm_b[:, :], bass.AP(tensor=IN['final_norm'].tensor, offset=0, ap=[[0, 128], [1, D]]), writes=['fnorm_b'])
    g.persist_mark = sb.off

    for l in range(nlayers):
        phase_mod(g, l)
        if 'inproj' in phases:
            phase_inproj(g, l)
        if 'mla' in phases:
            phase_mla(g, l)
        if 'sb' in phases:
            phase_sb(g, l)
        if 'nsa' in phases:
            phase_nsa(g, l)
        if 'outproj' in phases:
            phase_outproj(g, l, last=(l == nlayers - 1))
    S.barrier()
    S.emit()
    return nc


_NC_CACHE = {}


def kernel(**inputs):
    inp = {k: np.asarray(v) for k, v in inputs.items()}
    maps = prep_inputs(inp)
    if 'nc' not in _NC_CACHE:
        _NC_CACHE['nc'] = build()
    nc = _NC_CACHE['nc']
    res = run_bass_kernel_spmd(nc, maps, core_ids=list(range(8)))
    out = np.stack([np.asarray(r['out'], dtype=np.float32) for r in res.results], axis=0)
    return out.reshape(8, S_LEN, D)
```

```python
import numpy as np
import concourse.bass as bass
import concourse.mybir as mybir
from concourse.bass_utils import run_bass_kernel_spmd

F32 = mybir.dt.float32
BF16 = mybir.dt.bfloat16
I32 = mybir.dt.int32
AF = mybir.ActivationFunctionType
ALU = mybir.AluOpType
AX = mybir.AxisListType

ENGS = ['pe', 'act', 'dve', 'pool', 'sp']
SEM_CH = 8192
N_DMA_SEMS = 24

S_LEN = 4096
NT = 32
D = 1024
DEPTH = 2
D_IN = 3506
EPS = 1e-6
PI = float(np.pi)

P_R64, P_VC, P_VS, P_VW, P_SQ, P_SK, P_SV, P_CQ, P_CKV, P_KR, P_Z, P_GATE = (
    0, 768, 896, 1024, 1152, 1472, 1792, 2112, 2304, 2432, 2464, 3488)


class Sched:
    def __init__(self, nc):
        self.nc = nc
        self.ops = {e: [] for e in ENGS}
        self.cnt = {e: 0 for e in ENGS}
        self.waited = {e: {} for e in ENGS}
        self.waited_dma = {e: set() for e in ENGS}
        self.sems = {}
        self.dma_sems = [nc.alloc_semaphore(f"dsem{i}") for i in range(N_DMA_SEMS)]
        self.dma_uses = [0] * N_DMA_SEMS
        self.dma_last = [None] * N_DMA_SEMS
        self.dma_rr = 0
        self.lastw = {}
        self.readers = {}
        self.out_tokens = []

    def _sem(self, eng, idx):
        ch = (idx - 1) // SEM_CH
        k = (eng, ch)
        if k not in self.sems:
            self.sems[k] = self.nc.alloc_semaphore(f"s_{eng}_{ch}")
        return self.sems[k], (idx - 1) % SEM_CH + 1

    def _need(self, eng, tok, waits):
        if tok is None:
            return
        if tok[0] == 'dma':
            if tok in self.waited_dma[eng]:
                return
            self.waited_dma[eng].add(tok)
            waits.append((self.dma_sems[tok[1]], tok[2]))
        else:
            f, idx = tok
            if f == eng and eng in ('pe', 'sp'):
                return
            if self.waited[eng].get(f, 0) >= idx:
                return
            self.waited[eng][f] = idx
            waits.append(self._sem(f, idx))

    def _deps(self, eng, reads, writes):
        waits = []
        for k in reads:
            self._need(eng, self.lastw.get(k), waits)
            if k.startswith('ps'):
                for t in self.readers.get(k, ()):
                    if t[0] != eng:
                        self._need(eng, t, waits)
        for k in writes:
            self._need(eng, self.lastw.get(k), waits)
            for t in self.readers.get(k, ()):
                self._need(eng, t, waits)
        return waits

    def _commit(self, tok, reads, writes):
        for k in reads:
            self.readers.setdefault(k, []).append(tok)
        for k in writes:
            self.lastw[k] = tok
            self.readers[k] = []

    def op(self, eng, fn, reads=(), writes=()):
        waits = self._deps(eng, reads, writes)
        self.cnt[eng] += 1
        idx = self.cnt[eng]
        sem, val = self._sem(eng, idx)
        self.ops[eng].append((fn, waits, sem, 1))
        self._commit((eng, idx), reads, writes)

    def dma(self, out, in_, reads=(), writes=(), eng='sp', is_output=False, slow=False):
        waits = self._deps(eng, reads, writes)
        slot = self.dma_rr
        self.dma_rr = (self.dma_rr + 1) % N_DMA_SEMS
        self._need(eng, self.dma_last[slot], waits)
        self.dma_uses[slot] += 1
        tok = ('dma', slot, 16 * self.dma_uses[slot])
        self.dma_last[slot] = tok
        self.ops[eng].append((lambda e: e.dma_start(out=out, in_=in_, allow_slow_non_contiguous=slow), waits, self.dma_sems[slot], 16))
        self._commit(tok, reads, writes)
        if is_output:
            self.out_tokens.append(tok)

    def barrier(self):
        for e in ENGS:
            waits = []
            for f in ENGS:
                if f not in ('sp',) and self.cnt[f] > 0:
                    self._need(e, (f, self.cnt[f]), waits)
            for s in range(N_DMA_SEMS):
                self._need(e, self.dma_last[s], waits)
            if waits:
                self.ops[e].append((None, waits, None, 0))
        self.lastw = {}
        self.readers = {}

    def emit(self):
        nc = self.nc
        waits = []
        for t in self.out_tokens:
            self._need('sp', t, waits)
        for s in range(N_DMA_SEMS):
            self._need('sp', self.dma_last[s], waits)
        for e in ENGS:
            if e != 'sp' and self.cnt[e] > 0:
                self._need('sp', (e, self.cnt[e]), waits)
        self.ops['sp'].append((None, waits, None, 0))
        with nc.Block() as block:
            def run(e):
                def body(eng):
                    for fn, waits, sem, inc in self.ops[e]:
                        for (s, v) in waits:
                            eng.wait_ge(s, v)
                        if fn is not None:
                            ins = fn(eng)
                            ins.then_inc(sem, inc)
                return body
            block.tensor(run('pe'))
            block.scalar(run('act'))
            block.vector(run('dve'))
            block.gpsimd(run('pool'))
            block.sync(run('sp'))


def fap(base, dims):
    return bass.AP(tensor=base.tensor, offset=base.offset, ap=[list(base.ap[0])] + [list(d) for d in dims])


class SB:
    def __init__(self, nc):
        self.nc = nc
        self.off = 17408
        self.n = 0

    def alloc(self, name, shape, dtype):
        esz = 4 if dtype in (F32, I32) else 2
        free = int(np.prod(shape[1:])) * esz
        self.off = (self.off + 63) // 64 * 64
        self.n += 1
        t = self.nc.alloc_sbuf_tensor_at(f"{name}_{self.n}", list(shape), dtype, offset=self.off)
        self.off += free
        assert self.off <= 228 * 1024 - 512, (name, self.off)
        return t


def host_consts():
    c = {}
    c['ident'] = np.eye(128, dtype=np.float32)
    k = np.arange(128)[:, None]
    q = np.arange(512)[None, :]
    c['m_causal'] = (np.stack([((128 * i + k) <= q) for i in range(4)], 1).astype(np.float32) - 1.0) * 30000.0
    c['m_strict'] = (np.stack([((128 * i + k) < q) for i in range(4)], 1).astype(np.float32) - 1.0) * 30000.0
    c['m_win'] = (np.stack([(((128 * o + k) <= q) & ((128 * o + k) > q - 512)) for o in range(-4, 4)], 1).astype(np.float32) - 1.0) * 30000.0
    c['tri'] = (k >= np.arange(128)[None, :]).astype(np.float32)
    qq = np.arange(128)[None, :]
    c['mtri'] = (np.concatenate([(k <= qq), (k < qq), (k > qq)], axis=1).astype(np.float32) - 1.0) * 30000.0
    inv64 = (10000.0 ** (-np.arange(32, dtype=np.float32) / 32)).astype(np.float32)
    inv32 = (10000.0 ** (-np.arange(16, dtype=np.float32) / 16)).astype(np.float32)
    c['inv'] = np.broadcast_to(np.concatenate([inv64, inv32])[None, :], (128, 48)).astype(np.float32).copy()
    m = np.arange(503)[None, :]
    c['mbig'] = ((16 * (m - 248) + 31) <= k).astype(np.float32)
    r = np.arange(126)[None, :] - 62
    cb = k // 64
    invalid = r > cb
    f_cur = r == cb
    f_prev = r == cb - 1
    keep = (~invalid & ~f_cur & ~f_prev).astype(np.float32)
    add = np.where(invalid, -1e9, 0.0) + np.where(f_cur, 1.0e9, 0.0) + np.where(f_prev, 1.1e9, 0.0)
    c['selkeep'] = keep
    c['seladd'] = add.astype(np.float32)
    oh = np.zeros((128, S_LEN), np.float32)
    oh[64 + np.arange(S_LEN) // 64, np.arange(S_LEN)] = 1.0
    c['onehot'] = oh
    return c


CONST_SHAPES = {'ident': [128, 128], 'm_causal': [128, 4, 512], 'm_strict': [128, 4, 512], 'm_win': [128, 8, 512],
                'tri': [128, 128], 'mtri': [128, 384], 'inv': [128, 48], 'mbig': [128, 503], 'selkeep': [128, 126], 'seladd': [128, 126],
                'onehot': [128, S_LEN]}

IN_SHAPES = {
    'x': ([S_LEN, D], F32), 'c_row': ([1, D], F32), 'pos_pt': ([128, NT], I32),
    'ada_wT': ([DEPTH, 3 * D, D], F32), 'ada_bc': ([DEPTH, 128, 24], F32), 'norm_g': ([DEPTH, 128, 8], F32),
    'w_in': ([DEPTH, D, D_IN], F32), 'pos_kT': ([DEPTH, 64, 32], F32), 'pos_vT': ([DEPTH, 64, 32], F32),
    'ck_w1': ([DEPTH, 2048, 128], F32), 'ck_w2': ([DEPTH, 128, 64], F32),
    'cv_w1': ([DEPTH, 2048, 128], F32), 'cv_w2': ([DEPTH, 128, 64], F32),
    'q_norm': ([DEPTH, 192, 1], F32), 'w_uq': ([DEPTH, 192, 480], F32),
    'kv_norm': ([DEPTH, 128, 1], F32), 'w_ukv': ([DEPTH, 128, 640], F32),
    'w_out': ([DEPTH, D, D], F32), 'final_norm': ([1, D], F32),
}


def prep_inputs(inp):
    w = {}
    sp = np.cumsum([0, 384, 128, 128, 128, 128, 128, 128, 18, 384, 320, 320, 320, 320, 192, 128, 32, 320])
    names = ['n_q', 'n_kc', 'n_vc', 'n_ks', 'n_vs', 'n_kw', 'n_vw', 'n_gate', 'n_z', 's_q', 's_k', 's_v', 's_z',
             'm_cq', 'm_ckv', 'm_kr', 'm_z']
    rng = {n: np.arange(sp[i], sp[i + 1]) for i, n in enumerate(names)}
    order = ['n_q', 'n_kc', 'n_ks', 'n_kw', 'n_vc', 'n_vs', 'n_vw', 's_q', 's_k', 's_v', 'm_cq', 'm_ckv', 'm_kr',
             'n_z', 's_z', 'm_z', 'n_gate']
    perm = np.concatenate([rng[n] for n in order])
    shared = {
        'ada_wT': np.ascontiguousarray(inp['ada_w'].transpose(0, 2, 1)),
        'ada_bc': np.ascontiguousarray(inp['ada_b'].reshape(DEPTH, 24, 128).transpose(0, 2, 1)),
        'norm_g': np.ascontiguousarray(inp['norm_g'].reshape(DEPTH, 8, 128).transpose(0, 2, 1)),
        'w_in': np.ascontiguousarray(inp['w_in'][:, :, perm]),
        'pos_kT': np.ascontiguousarray(inp['nsa_pos_k'].transpose(0, 2, 1)),
        'pos_vT': np.ascontiguousarray(inp['nsa_pos_v'].transpose(0, 2, 1)),
        'ck_w1': np.ascontiguousarray(inp['nsa_ck_w1']), 'ck_w2': np.ascontiguousarray(inp['nsa_ck_w2']),
        'cv_w1': np.ascontiguousarray(inp['nsa_cv_w1']), 'cv_w2': np.ascontiguousarray(inp['nsa_cv_w2']),
        'q_norm': np.ascontiguousarray(inp['mla_q_norm'].reshape(DEPTH, 192, 1)),
        'kv_norm': np.ascontiguousarray(inp['mla_kv_norm'].reshape(DEPTH, 128, 1)),
        'w_out': np.ascontiguousarray(inp['w_out']),
        'final_norm': np.ascontiguousarray(inp['final_norm'].reshape(1, D)),
    }
    uq = inp['mla_w_uq'].reshape(DEPTH, 192, 5, 96)
    shared['w_uq'] = np.ascontiguousarray(np.concatenate(
        [uq[..., :64].reshape(DEPTH, 192, 320), uq[..., 64:].reshape(DEPTH, 192, 160)], axis=-1))
    ukv = inp['mla_w_ukv'].reshape(DEPTH, 128, 5, 128)
    shared['w_ukv'] = np.ascontiguousarray(np.concatenate(
        [ukv[..., :64].reshape(DEPTH, 128, 320), ukv[..., 64:].reshape(DEPTH, 128, 320)], axis=-1))
    shared.update(host_consts())
    maps = []
    for b in range(8):
        m = dict(shared)
        m['x'] = np.ascontiguousarray(inp['x'][b])
        m['c_row'] = np.ascontiguousarray(inp['c'][b].reshape(1, D))
        m['pos_pt'] = np.ascontiguousarray(inp['positions'][b].reshape(NT, 128).T.astype(np.int32))
        maps.append(m)
    return maps


def phase_mod(g, l):
    S, sb, IN, DR = g.S, g.sb, g.IN, g.DR
    S.barrier()
    sb.off = g.persist_mark
    scb = sb.alloc("scb", [128, D], F32)
    gcol = sb.alloc("gcol", [128, 8], F32)
    bcol = sb.alloc("bcol", [128, 24], F32)
    modc = sb.alloc("modc", [128, 24], F32)
    aw = [sb.alloc(f"aw{i}", [128, D], F32) for i in range(2)]
    jf = sb.alloc("jf", [128, D], F32)
    S.dma(scb[:, :], bass.AP(tensor=IN['c_row'].tensor, offset=0, ap=[[0, 128], [1, D]]), writes=['scb'])
    S.dma(gcol[:, :], IN['norm_g'][l], writes=['gcol'])
    S.dma(bcol[:, :], IN['ada_bc'][l], writes=['bcol'])
    S.op('act', lambda e: e.activation(out=scb[:, :], in_=scb[:, :], func=AF.Silu), reads=['scb'], writes=['scb'])
    S.op('dve', lambda e: e.memset(modc[:, :], 0.0), writes=['modc'])
    for ch in range(24):
        b = ch % 2
        S.dma(aw[b][:, :], IN['ada_wT'][l, ch * 128:(ch + 1) * 128, :], writes=[f'aw{b}'])
        S.op('dve', lambda e, b=b, ch=ch: e.scalar_tensor_tensor(out=jf[:, :], in0=aw[b][:, :], scalar=1.0, in1=scb[:, :], op0=ALU.mult, op1=ALU.mult,
                                                               accum_out=modc[:, ch:ch + 1]),
             reads=[f'aw{b}', 'scb', 'modc'], writes=['jf', f'modc{ch}'])
    mk = [f'modc{ch}' for ch in range(24)]
    S.op('dve', lambda e: e.tensor_tensor(out=modc[:, :], in0=modc[:, :], in1=bcol[:, :], op=ALU.add), reads=mk + ['bcol'], writes=['modc'])
    S.op('dve', lambda e: e.tensor_copy(out=g.modS[:, :], in_=modc[:, 0:8]), reads=['modc'], writes=['modS'])
    S.op('dve', lambda e: e.scalar_tensor_tensor(out=g.modA[:, :], in0=modc[:, 8:16], scalar=1.0, in1=gcol[:, :], op0=ALU.add, op1=ALU.mult),
         reads=['modc', 'gcol'], writes=['modA'])
    S.dma(DR['GROW'].rearrange("o (c p) -> p (o c)", p=128), modc[:, 16:24], reads=['modc'], writes=['GROW'], slow=True)
    S.dma(g.gate_b[:, :], bass.AP(tensor=DR['GROW'].tensor, offset=0, ap=[[0, 128], [1, D]]), reads=['GROW'], writes=['gate_b'])


def rope_ops(S, engs, x1, x2, cos, sin, o1, o2, t, rk, wk, tk):
    ea, eb = engs
    S.op(ea, lambda e: e.tensor_tensor(out=t[0], in0=x1, in1=cos, op=ALU.mult), reads=rk, writes=[tk[0]])
    S.op(ea, lambda e: e.tensor_tensor(out=t[1], in0=x2, in1=sin, op=ALU.mult), reads=rk, writes=[tk[1]])
    S.op(ea, lambda e: e.tensor_tensor(out=o1, in0=t[0], in1=t[1], op=ALU.subtract), reads=[tk[0], tk[1]], writes=[wk + '.o1'])
    S.op(eb, lambda e: e.tensor_tensor(out=t[2], in0=x2, in1=cos, op=ALU.mult), reads=rk, writes=[tk[2]])
    S.op(eb, lambda e: e.tensor_tensor(out=t[3], in0=x1, in1=sin, op=ALU.mult), reads=rk, writes=[tk[3]])
    S.op(eb, lambda e: e.tensor_tensor(out=o2, in0=t[2], in1=t[3], op=ALU.add), reads=[tk[2], tk[3]], writes=[wk + '.o2'])


def phase_inproj(g, l):
    import os
    CUT = int(os.environ.get('KCUT', '99'))
    NTT = int(os.environ.get('KNT', str(NT)))
    S, sb, IN, DR, C, ps, psb = g.S, g.sb, g.IN, g.DR, g.C, g.ps, g.psb
    S.barrier()
    sb.off = g.persist_mark
    W = sb.alloc("W", [128, 8, D_IN], BF16)
    proj = [sb.alloc(f"proj{i}", [128, D_IN], F32) for i in range(2)]
    Wuq = sb.alloc("Wuq", [128, 2, 480], BF16)
    Wukv = sb.alloc("Wukv", [128, 640], BF16)
    nrm = sb.alloc("nrm", [128, 3], F32)
    xt = [sb.alloc(f"xt{i}", [128, D], F32) for i in range(2)]
    junk = sb.alloc("junk", [128, D], BF16)
    xn = [sb.alloc(f"xn{i}", [128, D], BF16) for i in range(2)]
    hT = [sb.alloc(f"hT{i}", [128, 8, 128], BF16) for i in range(2)]
    htmp = sb.alloc("htmp", [128, 8, 128], F32)
    rt = [sb.alloc(f"rt{i}", [128, 12, 32], F32) for i in range(4)]
    TS = [sb.alloc(f"TS{i}", [128, 1984], BF16) for i in range(2)]
    FT = [sb.alloc(f"FT{i}", [128, 16, 128], BF16) for i in range(2)]
    vsw = [sb.alloc(f"vsw{i}", [128, 256], BF16) for i in range(2)]
    svt = [sb.alloc(f"svt{i}", [128, 320], BF16) for i in range(2)]
    zt = [sb.alloc(f"zt{i}", [128, 1024], F32) for i in range(2)]
    graw = sb.alloc("graw", [128, NT, 18], F32)
    ss = sb.alloc("ss", [128, NT, 3], F32)
    rs = sb.alloc("rs", [128, NT, 3], F32)
    qnb = [sb.alloc(f"qnb{i}", [128, 3, 128], BF16) for i in range(2)]
    knb = [sb.alloc(f"knb{i}", [128, 3, 128], BF16) for i in range(2)]
    mvb = [sb.alloc(f"mvb{i}", [128, 320], BF16) for i in range(2)]
    qrb = [sb.alloc(f"qrb{i}", [128, 256], BF16) for i in range(2)]
    qrT = [sb.alloc(f"qrT{i}", [128, 2, 128], BF16) for i in range(2)]
    ident = C['ident']
    cos64, sin64 = g.tabs['64']
    cos32, sin32 = g.tabs['32']

    for k in range(8):
        b = k % 2
        S.dma(proj[b][:, :], IN['w_in'][l, k * 128:(k + 1) * 128, :], writes=[f'proj{b}'] + [f'proj{b}.{cc}' for cc in range(7)])
        if k % 3 == 2:
            S.op('act', lambda e, b=b, k=k: e.copy(out=W[:, k, :], in_=proj[b][:, :]), reads=[f'proj{b}'], writes=[f'W{k}'])
        else:
            S.op(['dve', 'pool'][k % 3], lambda e, b=b, k=k: e.tensor_copy(out=W[:, k, :], in_=proj[b][:, :]), reads=[f'proj{b}'], writes=[f'W{k}'])
    S.dma(nrm[:, 0:1], IN['q_norm'][l, 0:128, :], writes=['nrm'])
    S.dma(nrm[0:64, 1:2], IN['q_norm'][l, 128:192, :], writes=['nrm'])
    S.dma(nrm[:, 2:3], IN['kv_norm'][l], writes=['nrm'])
    S.dma(xt[0][:, 0:480], IN['w_uq'][l, 0:128, :], writes=['xt0'])
    S.dma(xt[0][0:64, 480:960], IN['w_uq'][l, 128:192, :], writes=['xt0'])
    S.dma(xt[1][:, 0:640], IN['w_ukv'][l], writes=['xt1'])
    S.op('dve', lambda e: e.tensor_scalar(out=Wuq[:, 0, :], in0=xt[0][:, 0:480], scalar1=nrm[:, 0:1], scalar2=None, op0=ALU.mult),
         reads=['xt0', 'nrm'], writes=['Wuq'])
    S.op('dve', lambda e: e.tensor_scalar(out=Wuq[0:64, 1, :], in0=xt[0][0:64, 480:960], scalar1=nrm[0:64, 1:2], scalar2=None, op0=ALU.mult),
         reads=['xt0', 'nrm'], writes=['Wuq'])
    S.op('dve', lambda e: e.tensor_scalar(out=Wukv[:, :], in0=xt[1][:, 0:640], scalar1=nrm[:, 2:3], scalar2=None, op0=ALU.mult),
         reads=['xt1', 'nrm'], writes=['Wukv'])
    S.op('dve', lambda e: e.memset(ss[:, :, :], 0.0), writes=['ss'])
    for i_ in range(2):
        S.op('pool', lambda e, i_=i_: e.memset(TS[i_][:, 1888:1984], 0.0), writes=[f'TS{i_}.pad'])
        S.op('pool', lambda e, i_=i_: e.memset(qrb[i_][:, 160:256], 0.0), writes=[f'qrb{i_}.pad'])
    WK = [f'W{k}' for k in range(8)]
    xsrc = IN['x'] if l == 0 else DR['X1']
    chunks = [(c0, min(512, D_IN - c0)) for c0 in range(0, D_IN, 512)]

    rtk = [sb.alloc(f"rtk{i}", [128, 16], F32) for i in range(4)]
    rtq = [sb.alloc(f"rtq{i}", [128, 5, 16], F32) for i in range(4)]
    mhalf = sb.alloc("mhalf", [128, 4], F32)
    S.op('pool', lambda e: e.memset(mhalf[:, :], -0.5), writes=['mhalf'])

    def st1a(tt):
        b = tt % 2
        tts = slice(tt * 128, (tt + 1) * 128)
        S.dma(xt[b][:, :], xsrc[tts, :], writes=[f'xt{b}'], eng='pool')
        S.op('act', lambda e: e.activation(out=junk[:, :], in_=xt[b][:, :], func=AF.Square, accum_out=ss[:, tt, 0:1]),
             reads=[f'xt{b}', 'ss'], writes=['junk', f'ss{tt}.0'])
        S.op('dve', lambda e: e.tensor_scalar(out=rs[:, tt, 0:1], in0=ss[:, tt, 0:1], scalar1=1.0 / D, scalar2=EPS, op0=ALU.mult, op1=ALU.add),
             reads=[f'ss{tt}.0'], writes=[f'rs{tt}.0'])
        S.op('pool', lambda e: e.tensor_tensor(out=rs[:, tt, 0:1], in0=rs[:, tt, 0:1], in1=mhalf[:, 0:1], op=ALU.pow), reads=[f'rs{tt}.0', 'mhalf'], writes=[f'rs{tt}.0'])
        S.op('act', lambda e: e.activation(out=xn[b][:, :], in_=xt[b][:, :], func=AF.Copy, scale=rs[:, tt, 0:1]),
             reads=[f'xt{b}', f'rs{tt}.0'], writes=[f'xn{b}'])

    def st1b(tt):
        b = tt % 2
        for c in range(8):
            S.op('pe', lambda e, c=c: e.transpose(psb[7][:, c * 128:(c + 1) * 128], xn[b][:, c * 128:(c + 1) * 128], ident[:, :]),
                 reads=[f'xn{b}', 'c_ident'], writes=['ps7'])
        S.op('dve', lambda e: e.tensor_tensor(out=htmp[:, :, :], in0=psb[7][:, :].rearrange("p (c t) -> p c t", t=128),
                                              in1=fap(g.modA[:, :], [(1, 8), (0, 128)]), op=ALU.mult),
             reads=['ps7', 'modA'], writes=['htmp'])
        S.op('dve', lambda e: e.tensor_tensor(out=hT[b][:, :, :], in0=htmp[:, :, :], in1=fap(g.modS[:, :], [(1, 8), (0, 128)]), op=ALU.add),
             reads=['htmp', 'modS'], writes=[f'hT{b}'])

    def st2a(tt):
        b = tt % 2
        for cc, (c0, n) in enumerate(chunks):
            bk = cc % 2
            for k in range(8):
                S.op('pe', lambda e, k=k, c0=c0, n=n, bk=bk: e.matmul(ps[bk][:, 0:n], lhsT=hT[b][:, k, :], rhs=W[:, k, c0:c0 + n],
                                                                    start=(k == 0), stop=(k == 7)),
                     reads=[f'hT{b}', WK[k]], writes=[f'ps{bk}'])
            S.op('act', lambda e, c0=c0, n=n, bk=bk: e.copy(out=proj[b][:, c0:c0 + n], in_=ps[bk][:, 0:n]),
                 reads=[f'ps{bk}'], writes=[f'proj{b}.{cc}'])

    def st2b(tt):
        b = tt % 2
        R = proj[b]
        pk = [f'proj{b}.{cc}' for cc in range(7)]
        R3 = R[:, 0:768].rearrange("p (h d) -> p h d", d=64)
        T3 = TS[b][:, 0:768].rearrange("p (h d) -> p h d", d=64)
        cb = fap(cos64[:, tt, :], [(0, 12), (1, 32)])
        snb = fap(sin64[:, tt, :], [(0, 12), (1, 32)])
        rope_ops(S, ('dve', 'pool'), R3[:, :, 0:32], R3[:, :, 32:64], cb, snb, T3[:, :, 0:32], T3[:, :, 32:64],
                 [r[:, :, :] for r in rt], [f'proj{b}.0', f'proj{b}.1', 'cos64', 'sin64'], f'TS{b}.r64', ['rt0', 'rt1', 'rt2', 'rt3'])
        S.op('pool', lambda e: e.tensor_copy(out=TS[b][:, 768:896], in_=proj[b][:, P_VC:P_VC + 128]), reads=pk, writes=[f'TS{b}.vc'])
        S.op('pool', lambda e: e.tensor_copy(out=TS[b][:, 896:1536], in_=proj[b][:, P_SQ:P_SQ + 640]), reads=pk, writes=[f'TS{b}.sqk'])
        S.op('pool', lambda e: e.tensor_copy(out=vsw[b][:, :], in_=proj[b][:, P_VS:P_VS + 256]), reads=pk, writes=[f'vsw{b}'])
        S.op('pool', lambda e: e.tensor_copy(out=svt[b][:, :], in_=proj[b][:, P_SV:P_SV + 320]), reads=pk, writes=[f'svt{b}'])
        S.op('act', lambda e: e.activation(out=zt[b][:, :], in_=proj[b][:, P_Z:P_Z + 1024], func=AF.Silu), reads=pk, writes=[f'zt{b}'])
        S.op('pool', lambda e: e.tensor_copy(out=graw[:, tt, :], in_=proj[b][:, P_GATE:P_GATE + 18]), reads=pk, writes=[f'graw{tt}'])
        S.op('act', lambda e: e.activation(out=junk[:, 0:192], in_=proj[b][:, P_CQ:P_CQ + 192], func=AF.Square, accum_out=ss[:, tt, 1:2]),
             reads=pk + ['ss'], writes=['junk', f'ss{tt}.1'])
        S.op('act', lambda e: e.activation(out=junk[:, 0:128], in_=proj[b][:, P_CKV:P_CKV + 128], func=AF.Square, accum_out=ss[:, tt, 2:3]),
             reads=pk + ['ss'], writes=['junk', f'ss{tt}.2'])
        for j, dn in ((1, 192.0), (2, 128.0)):
            S.op('dve', lambda e, j=j, dn=dn: e.tensor_scalar(out=rs[:, tt, j:j + 1], in0=ss[:, tt, j:j + 1], scalar1=1.0 / dn, scalar2=EPS, op0=ALU.mult, op1=ALU.add),
                 reads=[f'ss{tt}.{j}'], writes=[f'rs{tt}.{j}'])
        S.op('pool', lambda e: e.tensor_tensor(out=rs[:, tt, 1:3], in0=rs[:, tt, 1:3], in1=mhalf[:, 1:3], op=ALU.pow),
             reads=[f'rs{tt}.1', f'rs{tt}.2', 'mhalf'], writes=[f'rs{tt}.1', f'rs{tt}.2'])
        S.op('dve', lambda e: e.tensor_scalar(out=TS[b][:, 1536:1728], in0=proj[b][:, P_CQ:P_CQ + 192], scalar1=rs[:, tt, 1:2], scalar2=None, op0=ALU.mult),
             reads=pk + [f'rs{tt}.1'], writes=[f'TS{b}.cq'])
        S.op('dve', lambda e: e.tensor_scalar(out=TS[b][:, 1728:1856], in0=proj[b][:, P_CKV:P_CKV + 128], scalar1=rs[:, tt, 2:3], scalar2=None, op0=ALU.mult),
             reads=pk + [f'rs{tt}.2'], writes=[f'TS{b}.ckv'])
        rope_ops(S, ('dve', 'dve'), R[:, P_KR:P_KR + 16], R[:, P_KR + 16:P_KR + 32], cos32[:, tt, :], sin32[:, tt, :],
                 TS[b][:, 1856:1872], TS[b][:, 1872:1888], [r[:, :] for r in rtk], pk + ['cos32', 'sin32'], f'TS{b}.kr', ['rtk0', 'rtk1', 'rtk2', 'rtk3'])

    def st3a(tt):
        b = tt % 2
        tts = slice(tt * 128, (tt + 1) * 128)
        tskeys = [f'TS{b}.r64.o1', f'TS{b}.r64.o2', f'TS{b}.vc', f'TS{b}.sqk', f'TS{b}.cq', f'TS{b}.ckv', f'TS{b}.kr.o1', f'TS{b}.kr.o2', f'TS{b}.pad']
        tsrc = [(i * 128, 128) for i in range(12)] + [(1536, 128), (1664, 128), (1728, 128), (1856, 128)]
        for i, (c0, w) in enumerate(tsrc):
            bk = 3 + i // 8
            sl = i % 8
            S.op('pe', lambda e, c0=c0, w=w, bk=bk, sl=sl: e.transpose(psb[bk][0:w, sl * 128:(sl + 1) * 128], TS[b][:, c0:c0 + w], ident[:, :]),
                 reads=tskeys + ['c_ident'], writes=[f'ps{bk}'])
        S.op('dve', lambda e: e.tensor_copy(out=FT[b][:, 0:8, :], in_=psb[3][:, :].rearrange("p (c t) -> p c t", t=128)),
             reads=['ps3'], writes=[f'FT{b}.a'])
        S.op('act', lambda e: e.copy(out=FT[b][:, 8:16, :], in_=psb[4][:, :].rearrange("p (c t) -> p c t", t=128)),
             reads=['ps4'], writes=[f'FT{b}.b'])
        S.dma(DR['R64T'].rearrange("(i p) t -> p i t", p=128)[:, :, tts], FT[b][:, 0:6, :], reads=[f'FT{b}.a'])
        S.dma(DR['VCT'][:, tts], FT[b][:, 6, :], reads=[f'FT{b}.a'])
        S.dma(DR['SQKT'][0:128, tts], FT[b][:, 7, :], reads=[f'FT{b}.a'])
        S.dma(DR['SQKT'].rearrange("(i p) t -> p i t", p=128)[:, 1:5, tts], FT[b][:, 8:12, :], reads=[f'FT{b}.b'])
        S.dma(DR['MKPE'][:, tts], FT[b][0:32, 15, :], reads=[f'FT{b}.b'])
        S.dma(DR['VSW'][tts, :], vsw[b][:, :], reads=[f'vsw{b}'])
        S.dma(DR['SV'][tts, :], svt[b][:, :], reads=[f'svt{b}'])
        S.dma(DR['Z'][tts, :], zt[b][:, :], reads=[f'zt{b}'], eng='pool')

    def st3b(tt):
        b = tt % 2
        tts = slice(tt * 128, (tt + 1) * 128)
        cq0, cq1, ckvT = FT[b][:, 12, :], FT[b][0:64, 13, :], FT[b][:, 14, :]
        fk = [f'FT{b}.b']
        for grp, (c0, w) in enumerate(((0, 128), (128, 128), (256, 128))):
            S.op('pe', lambda e, grp=grp, c0=c0, w=w: e.matmul(ps[5][0:w, grp * 128:(grp + 1) * 128], lhsT=Wuq[:, 0, c0:c0 + w], rhs=cq0, start=True, stop=False),
                 reads=fk + ['Wuq'], writes=['ps5'])
            S.op('pe', lambda e, grp=grp, c0=c0, w=w: e.matmul(ps[5][0:w, grp * 128:(grp + 1) * 128], lhsT=Wuq[0:64, 1, c0:c0 + w], rhs=cq1, start=False, stop=True),
                 reads=fk + ['Wuq'], writes=['ps5'])
            S.op('pe', lambda e, grp=grp, c0=c0, w=w: e.matmul(ps[6][0:w, grp * 128:(grp + 1) * 128], lhsT=Wukv[:, c0:c0 + w], rhs=ckvT, start=True, stop=True),
                 reads=fk + ['Wukv'], writes=['ps6'])
        S.op('pe', lambda e: e.matmul(ps[2][:, 0:160], lhsT=cq0, rhs=Wuq[:, 0, 320:480], start=True, stop=False), reads=fk + ['Wuq'], writes=['ps2'])
        S.op('pe', lambda e: e.matmul(ps[2][:, 0:160], lhsT=cq1, rhs=Wuq[0:64, 1, 320:480], start=False, stop=True), reads=fk + ['Wuq'], writes=['ps2'])
        S.op('pe', lambda e: e.matmul(ps[2][:, 160:480], lhsT=ckvT, rhs=Wukv[:, 320:640], start=True, stop=True, skip_group_check=True),
             reads=fk + ['Wukv'], writes=['ps2'])
        S.op('act', lambda e: e.copy(out=qnb[b][:, :, :], in_=ps[5][:, 0:384].rearrange("p (c t) -> p c t", t=128)), reads=['ps5'], writes=[f'qnb{b}'])
        S.op('dve', lambda e: e.tensor_copy(out=knb[b][:, :, :], in_=ps[6][:, 0:384].rearrange("p (c t) -> p c t", t=128)), reads=['ps6'], writes=[f'knb{b}'])
        S.op('act', lambda e: e.copy(out=mvb[b][:, :], in_=ps[2][:, 160:480]), reads=['ps2'], writes=[f'mvb{b}'])
        Q3 = ps[2][:, 0:160].rearrange("p (h d) -> p h d", d=32)
        O3 = qrb[b][:, 0:160].rearrange("p (h d) -> p h d", d=32)
        rope_ops(S, ('dve', 'dve'), Q3[:, :, 0:16], Q3[:, :, 16:32], fap(cos32[:, tt, :], [(0, 5), (1, 16)]), fap(sin32[:, tt, :], [(0, 5), (1, 16)]),
                 O3[:, :, 0:16], O3[:, :, 16:32], [r[:, :, :] for r in rtq], ['ps2', 'cos32', 'sin32'], f'qrb{b}', ['rtq0', 'rtq1', 'rtq2', 'rtq3'])
        for nm, src in (('MQN', qnb[b]), ('MKN', knb[b])):
            S.dma(DR[nm][0:256, :].rearrange("(i p) t -> p i t", p=128)[:, :, tts], src[:, 0:2, :], reads=[f'{nm[1:].lower()}b{b}'])
            S.dma(DR[nm][256:320, tts], src[0:64, 2, :], reads=[f'{nm[1:].lower()}b{b}'])
        S.dma(DR['MV'][tts, :], mvb[b][:, :], reads=[f'mvb{b}'])

    def st3c(tt):
        b = tt % 2
        tts = slice(tt * 128, (tt + 1) * 128)
        for i, (c0, w) in enumerate(((0, 128), (128, 128))):
            S.op('pe', lambda e, c0=c0, w=w, i=i: e.transpose(psb[6][0:w, 768 + i * 128:768 + (i + 1) * 128], qrb[b][:, c0:c0 + w], ident[:, :]),
                 reads=[f'qrb{b}.o1', f'qrb{b}.o2', f'qrb{b}.pad', 'c_ident'], writes=['ps6'])
        S.op('act', lambda e: e.copy(out=qrT[b][:, :, :], in_=psb[6][:, 768:1024].rearrange("p (c t) -> p c t", t=128)), reads=['ps6'], writes=[f'qrT{b}'])
        S.dma(DR['MQR'][0:128, tts], qrT[b][:, 0, :], reads=[f'qrT{b}'])
        S.dma(DR['MQR'][128:160, tts], qrT[b][0:32, 1, :], reads=[f'qrT{b}'])

    st1a(0)
    if NTT > 1:
        st1a(1)
    st1b(0)
    for it in range(1, NTT + 4):
        if it + 1 < NTT:
            st1a(it + 1)
        if it < NTT:
            st1b(it)
        if 0 <= it - 1 < NTT:
            st2a(it - 1)
        if 0 <= it - 2 < NTT:
            st3a(it - 2)
        if 0 <= it - 1 < NTT:
            st2b(it - 1)
        if 0 <= it - 3 < NTT:
            st3b(it - 3)
        if 0 <= it - 4 < NTT:
            st3c(it - 4)
    S.op('act', lambda e: e.activation(out=graw[:, :, :], in_=graw[:, :, :], func=AF.Sigmoid), reads=[f'graw{tt}' for tt in range(NT)], writes=['gsig'])
    S.dma(DR['GATE'].rearrange("(t p) c -> p t c", p=128), graw[:, :, :], reads=['gsig'])


def attn_stream(g, jobs, scale, sbanks=(0, 1, 2), dummy=None):
    S, ps = g.S, g.ps
    flat = []
    for ji, job in enumerate(jobs):
        for ui, u in enumerate(job['units']):
            flat.append((ji, ui, u))
    prev = None
    first_in_bank = {}
    for ji in range(min(2, len(jobs))):
        if jobs[ji].get('pre'):
            jobs[ji]['pre']()
    bg = None
    LEAD = min(3, len(sbanks) - 1)
    pend = []
    NF = len(flat)
    for i in range(NF + LEAD):
        cur = flat[i] if i < NF else None
        if cur is not None:
            ji, ui, u = cur
            if ui == 0 and ji + 2 < len(jobs) and jobs[ji + 2].get('pre'):
                jobs[ji + 2]['pre']()
            if ui == 0 and jobs[ji].get('need_bg') and bg is not None:
                for _ in bg:
                    pass
                bg = None
            if ui == 3 and jobs[ji].get('bg'):
                bg = jobs[ji]['bg']()
            NB_ = len(sbanks)
            sbk = sbanks[i % NB_]
            pb = g.pbf[i % NB_]
            hm = u['mask'] is not None
            S.op('pe', lambda e, u=u, sbk=sbk, hm=hm: e.matmul(ps[sbk][:, :], lhsT=u['lhsT'], rhs=u['rhs'], start=True, stop=True),
                 reads=u['rk'], writes=[f'ps{sbk}'])
            if hm:
                mj, map_ = u['mask']
                S.op('pe', lambda e, sbk=sbk, mj=mj, map_=map_: e.matmul(ps[sbk][:, mj * 128:(mj + 1) * 128], lhsT=g.C['ident'][:, :], rhs=map_, start=False, stop=True,
                                                                       skip_group_check=True),
                     reads=['c_mtri', 'c_ident'], writes=[f'ps{sbk}'])
            S.op('act', lambda e, sbk=sbk, pb=pb: e.activation(out=pb[:, :], in_=ps[sbk][:, :], func=AF.Exp, scale=scale),
                 reads=[f'ps{sbk}'], writes=[f'pbf{i % NB_}'])
            pend.append((i, cur))
        if dummy is not None and cur is not None:
            S.op('pe', lambda e: e.matmul(ps[dummy][:, 0:256], lhsT=g.C['ident'][:, :], rhs=g.C['mtri'][:, 0:256], start=True, stop=True), reads=['c_ident', 'c_mtri'], writes=[f'ps{dummy}'])
        if bg is not None and i % 7 == 1:
            try:
                next(bg)
            except StopIteration:
                bg = None
        if len(pend) > LEAD or (cur is None and pend):
            pi, (ji, ui, u) = pend.pop(0)
            pb = g.pbf[pi % len(sbanks)]
            ab = 3 + ji % 2
            for j in u['subs']:
                first = (ji, ) not in first_in_bank
                first_in_bank[(ji, )] = True
                S.op('pe', lambda e, pb=pb, u=u, j=j, ab=ab, first=first: e.matmul(ps[ab][:, j * 65:(j + 1) * 65], lhsT=pb[:, j * 128:(j + 1) * 128], rhs=u['vext'],
                                                                                 start=first, stop=False, skip_group_check=True),
                     reads=[f'pbf{pi % len(sbanks)}', u['vk']], writes=[f'ps{ab}'])
            if ui == len(jobs[ji]['units']) - 1:
                jobs[ji]['epi'](ab)
    if bg is not None:
        for _ in bg:
            pass


def phase_mla(g, l):
    S, sb, IN, DR, C, ps = g.S, g.sb, g.IN, g.DR, g.C, g.ps
    S.barrier()
    sb.off = g.persist_mark
    H = 5
    KT = [sb.alloc(f"mKT{h}", [96, S_LEN], BF16) for h in range(H)]
    VX = [sb.alloc(f"mVX{h}", [128, NT, 65], BF16) for h in range(H)]
    QT = [sb.alloc(f"mQT{i}", [96, 512], BF16) for i in range(4)]
    g.pbf = [sb.alloc(f"pbf{i}", [128, 512], BF16) for i in range(6)]
    zt = [sb.alloc(f"mz{i}", [128, 4, 320], F32) for i in range(2)]
    mixt = [sb.alloc(f"mmix{i}", [128, 4, 320], BF16) for i in range(2)]
    rec = [sb.alloc(f"mrec{i}", [128, 4], F32) for i in range(2)]
    dacc = sb.alloc("dacc", [128, 260], F32)
    for h in range(H):
        S.dma(KT[h][0:64, :], DR['MKN'][h * 64:(h + 1) * 64, :], writes=[f'mKT{h}a'])
        S.dma(KT[h][64:96, :], DR['MKPE'][:, :], writes=[f'mKT{h}b'])
        S.op('pool', lambda e, h=h: e.memset(VX[h][:, :, :], 1.0), writes=[f'mVX{h}'])
        S.dma(VX[h][:, :, 0:64], DR['MV'][:, h * 64:(h + 1) * 64].rearrange("(t p) d -> p t d", p=128), writes=[f'mVX{h}'])
    g.dbgout('D_KT', KT[0][:, 0:512], ['mKT0a', 'mKT0b'])
    g.dbgout('D_VX', VX[0][:, 0:4, :], ['mVX0'])
    scale = 96.0 ** -0.5
    jobs = []
    qi = 0
    for c in range(8):
        cs = slice(c * 512, (c + 1) * 512)
        zb = c % 2
        for h in range(H):
            q = qi % 4
            qi += 1

            def pre(h=h, c=c, cs=cs, zb=zb, q=q):
                if h == 0:
                    S.dma(zt[zb][:, :, :], DR['Z'][cs, 704:1024].rearrange("(j p) d -> p j d", p=128), writes=[f'mz{zb}'])
                S.dma(QT[q][0:64, :], DR['MQN'][h * 64:(h + 1) * 64, cs], writes=[f'mQT{q}a'], eng='pool')
                S.dma(QT[q][64:96, :], DR['MQR'][h * 32:(h + 1) * 32, cs], writes=[f'mQT{q}b'], eng='pool')
            units = []
            for kt in range(4 * c + 4):
                i = kt - 4 * c
                units.append(dict(lhsT=KT[h][:, kt * 128:(kt + 1) * 128], rhs=QT[q][:, :], rk=[f'mKT{h}a', f'mKT{h}b', f'mQT{q}a', f'mQT{q}b'],
                                  mask=((i, C['mtri'][:, 0:128]) if i >= 0 else None), mk='c_mtri',
                                  subs=(list(range(i, 4)) if i >= 0 else [0, 1, 2, 3]), vext=VX[h][:, kt, :], vk=f'mVX{h}'))

            def epi(ab, h=h, c=c, zb=zb):
                rb = rec[zb]
                acc = ps[ab][:, 0:260].rearrange("p (j e) -> p j e", e=65)
                if h == 0 and c == 0 and 'D_ACC' in g.dbg:
                    S.op('dve', lambda e: e.tensor_copy(out=dacc[:, :], in_=ps[ab][:, 0:260]), reads=[f'ps{ab}'], writes=['dacc'])
                    g.dbgout('D_ACC', dacc[:, :], ['dacc'])
                S.op('dve', lambda e: e.reciprocal(out=rb[:, :], in_=acc[:, :, 64]), reads=[f'ps{ab}'], writes=[f'mrec{zb}'])
                for j in range(4):
                    S.op('dve', lambda e, j=j: e.scalar_tensor_tensor(out=mixt[zb][:, j, h * 64:(h + 1) * 64], in0=acc[:, j, 0:64], scalar=rb[:, j:j + 1],
                                                                      in1=zt[zb][:, j, h * 64:(h + 1) * 64], op0=ALU.mult, op1=ALU.mult),
                         reads=[f'ps{ab}', f'mrec{zb}', f'mz{zb}'], writes=[f'mmix{zb}.{h}'])
                if h == H - 1:
                    S.dma(DR['MIX'][c * 512:(c + 1) * 512, 704:1024].rearrange("(j p) d -> p j d", p=128), mixt[zb][:, :, :],
                          reads=[f'mmix{zb}.{hh}' for hh in range(H)])
            jobs.append(dict(units=units, epi=epi, pre=pre))
    attn_stream(g, jobs, scale, sbanks=(0, 1, 2, 5, 6, 7))


def phase_sb(g, l):
    S, sb, IN, DR, C, ps = g.S, g.sb, g.IN, g.DR, g.C, g.ps
    S.barrier()
    sb.off = g.persist_mark
    H = 5
    KT = [sb.alloc(f"sKT{h}", [128, S_LEN], BF16) for h in range(H)]
    VV = [sb.alloc(f"sVV{h}", [128, NT, 64], BF16) for h in range(H)]
    QT = [sb.alloc(f"sQT{i}", [128, 512], BF16) for i in range(4)]
    e_sb = [sb.alloc(f"se{i}", [128, 512], F32) for i in range(3)]
    sp_bf = [sb.alloc(f"ssp{i}", [128, 512], BF16) for i in range(2)]
    x_sb = [sb.alloc(f"sx{i}", [128, 512], F32) for i in range(2)]
    a_bf = [sb.alloc(f"sa{i}", [128, 512], BF16) for i in range(2)]
    o_sb = [sb.alloc(f"so{i}", [128, 4, 64], F32) for i in range(2)]
    chist = [sb.alloc(f"sch{i}", [128, 33, 4], F32) for i in range(2)]
    ecar = [sb.alloc(f"sec{i}", [128, 33, 4], F32) for i in range(2)]
    zt = [sb.alloc(f"sz{i}", [128, 4, 320], F32) for i in range(2)]
    mixt = [sb.alloc(f"smix{i}", [128, 4, 320], BF16) for i in range(2)]
    tri = C['tri']
    ones_col = g.ones_bf[:, 0:1]
    for i_ in range(4):
        S.op('pool', lambda e, i_=i_: e.memset(QT[i_][64:128, :], 0.0), writes=[f'sQT{i_}.z'])
    for h in range(H):
        S.op(['pool', 'dve'][h % 2], lambda e, h=h: e.memset(KT[h][64:128, :], 0.0), writes=[f'sKT{h}.z'])
        S.dma(KT[h][0:64, :], DR['SQKT'][320 + h * 64:320 + (h + 1) * 64, :], writes=[f'sKT{h}'])
        S.dma(VV[h][:, :, :], DR['SV'][:, h * 64:(h + 1) * 64].rearrange("(t p) d -> p t d", p=128), writes=[f'sVV{h}'])
    jobs = []
    for c in range(8):
        for h in range(H):
            jobs.append((c, h))
    flat = []
    for ji, (c, h) in enumerate(jobs):
        U = 4 * c + 4
        for u in range(U):
            flat.append((ji, u, U, c, h, 4 * c + 3 - u))

    def pre(ji):
        c, h = jobs[ji]
        cs = slice(c * 512, (c + 1) * 512)
        if h == 0:
            S.dma(zt[c % 2][:, :, :], DR['Z'][cs, 384:704].rearrange("(j p) d -> p j d", p=128), writes=[f'sz{c % 2}'])
        S.dma(QT[ji % 4][0:64, :], DR['SQKT'][h * 64:(h + 1) * 64, cs], writes=[f'sQT{ji % 4}'])

    for ji in range(2):
        pre(ji)
    N = len(flat)

    gtmp = sb.alloc("sgtmp", [128, 4, 64], F32)

    def stA(n):
        ji, u, U, c, h, kt = flat[n]
        if u == 0 and ji + 2 < len(jobs):
            pre(ji + 2)
        zb = n % 2
        i = kt - 4 * c
        S.op('pe', lambda e: e.matmul(ps[zb][:, :], lhsT=KT[h][:, kt * 128:(kt + 1) * 128], rhs=QT[ji % 4][:, :], start=True, stop=True),
             reads=[f'sKT{h}', f'sKT{h}.z', f'sQT{ji % 4}', f'sQT{ji % 4}.z'], writes=[f'ps{zb}'])
        if i >= 0:
            S.op('pe', lambda e: e.matmul(ps[zb][:, i * 128:(i + 1) * 128], lhsT=C['ident'][:, :], rhs=C['mtri'][:, 128:256], start=False, stop=True, skip_group_check=True),
                 reads=['c_ident', 'c_mtri'], writes=[f'ps{zb}'])

    def stB1(n):
        ji, u, U, c, h, kt = flat[n]
        p = ji % 2
        eb, zb = e_sb[n % 3], n % 2
        if u == 0:
            S.op('pool', lambda e: e.memset(o_sb[p][:, :, :], 0.0), writes=[f'so{p}'])
            S.op('pool', lambda e: e.memset(chist[p][:, 0:5, :], 0.0), writes=[f'sch{p}'] + [f'sch{p}.{x}' for x in range(1, 5)])
            S.op('pool', lambda e: e.memset(ecar[p][:, 0, :], 1.0), writes=[f'sec{p}'])
        S.op('act', lambda e: e.activation(out=eb[:, :], in_=ps[zb][:, :], func=AF.Exp, scale=0.125), reads=[f'ps{zb}'], writes=[f'se{n % 3}'])

    def stB2(n):
        eb, spb = e_sb[n % 3], sp_bf[n % 2]
        S.op('act', lambda e: e.activation(out=spb[:, :], in_=eb[:, :], func=AF.Ln, bias=1.0), reads=[f'se{n % 3}'], writes=[f'ssp{n % 2}'])

    def stC(n):
        ji, u, U, c, h, kt = flat[n]
        p = ji % 2
        spb = sp_bf[n % 2]
        cb, kb = 2 + n % 2, 6 + n % 2
        S.op('pe', lambda e: e.matmul(ps[cb][:, :], lhsT=tri[:, :], rhs=spb[:, :], start=True, stop=True), reads=[f'ssp{n % 2}', 'c_tri'], writes=[f'ps{cb}'])
        j0 = max(kt - 4 * c, 0)
        for jj in range(j0, 4):
            S.op('pe', lambda e, jj=jj: e.matmul(ps[kb][:, jj:jj + 1], lhsT=spb[:, jj * 128:(jj + 1) * 128], rhs=ones_col, start=True, stop=True),
                 reads=[f'ssp{n % 2}', 'ones_bf'], writes=[f'ps{kb}'])
        S.op('dve', lambda e: e.tensor_tensor(out=chist[p][:, u + 1, j0:4], in0=chist[p][:, u, j0:4], in1=ps[kb][:, j0:4], op=ALU.add),
             reads=[f'ps{kb}', f'sch{p}', f'sch{p}.{u}'], writes=[f'sch{p}.{u + 1}'])

    def stEcar(n):
        ji, u, U, c, h, kt = flat[n]
        p = ji % 2
        S.op('act', lambda e: e.activation(out=ecar[p][:, u + 1, :], in_=chist[p][:, u + 1, :], func=AF.Exp, scale=-1.0), reads=[f'sch{p}.{u + 1}'], writes=[f'sec{p}.{u + 1}'])

    def stD(n):
        cb = 2 + n % 2
        xb, eb, ab_ = x_sb[n % 2], e_sb[n % 3], a_bf[n % 2]
        S.op('act', lambda e: e.activation(out=xb[:, :], in_=ps[cb][:, :], func=AF.Exp, scale=-1.0), reads=[f'ps{cb}'], writes=[f'sx{n % 2}'])
        S.op('pool', lambda e: e.tensor_tensor(out=ab_[:, 0:320], in0=eb[:, 0:320], in1=xb[:, 0:320], op=ALU.mult), reads=[f'se{n % 3}', f'sx{n % 2}'], writes=[f'sa{n % 2}.a'])
        S.op('dve', lambda e: e.tensor_tensor(out=ab_[:, 320:512], in0=eb[:, 320:512], in1=xb[:, 320:512], op=ALU.mult), reads=[f'se{n % 3}', f'sx{n % 2}'], writes=[f'sa{n % 2}.b'])

    def stF(n):
        ji, u, U, c, h, kt = flat[n]
        p = ji % 2
        i = kt - 4 * c
        act = list(range(max(i, 0), 4))
        ob = 4 + n % 2
        ab_ = a_bf[n % 2]
        ek = [f'sec{p}'] if u == 0 else [f'sec{p}.{u}']
        for jj in act:
            S.op('pe', lambda e, jj=jj: e.matmul(ps[ob][:, jj * 64:(jj + 1) * 64], lhsT=ab_[:, jj * 128:(jj + 1) * 128], rhs=VV[h][:, kt, :], start=True, stop=True),
                 reads=[f'sa{n % 2}.a', f'sa{n % 2}.b', f'sVV{h}'], writes=[f'ps{ob}'])
        if len(act) == 4:
            S.op('dve', lambda e: e.tensor_tensor(out=gtmp[:, :, :], in0=ps[ob][:, 0:256].rearrange("p (j d) -> p j d", d=64),
                                                  in1=fap(ecar[p][:, u, :], [(1, 4), (0, 64)]), op=ALU.mult),
                 reads=[f'ps{ob}'] + ek, writes=['sgtmp'])
            S.op('dve', lambda e: e.tensor_tensor(out=o_sb[p][:, :, :], in0=o_sb[p][:, :, :], in1=gtmp[:, :, :], op=ALU.add),
                 reads=['sgtmp', f'so{p}'], writes=[f'so{p}'])
        else:
            for jj in act:
                S.op('dve', lambda e, jj=jj: e.scalar_tensor_tensor(out=o_sb[p][:, jj, :], in0=ps[ob][:, jj * 64:(jj + 1) * 64], scalar=ecar[p][:, u, jj:jj + 1],
                                                                    in1=o_sb[p][:, jj, :], op0=ALU.mult, op1=ALU.add),
                     reads=[f'ps{ob}', f'so{p}'] + ek, writes=[f'so{p}'])
        if u == U - 1:
            zb = c % 2
            S.op('pool', lambda e: e.tensor_tensor(out=mixt[zb][:, :, h * 64:(h + 1) * 64], in0=o_sb[p][:, :, :], in1=zt[zb][:, :, h * 64:(h + 1) * 64], op=ALU.mult),
                 reads=[f'so{p}', f'sz{zb}'], writes=[f'smix{zb}.{h}'])
            if h == H - 1:
                S.dma(DR['MIX'][c * 512:(c + 1) * 512, 384:704].rearrange("(j p) d -> p j d", p=128), mixt[zb][:, :, :],
                      reads=[f'smix{zb}.{hh}' for hh in range(H)])

    stA(0)
    if N > 1:
        stA(1)
    for it in range(N + 1):
        if it < N:
            stB1(it)
        if it >= 1:
            stD(it - 1)
        if it < N:
            stB2(it)
        if it >= 1:
            stEcar(it - 1)
        if it < N:
            stC(it)
        if it + 2 < N:
            stA(it + 2)
        if it >= 1:
            stF(it - 1)


def phase_nsa(g, l):
    S, sb, IN, DR, C, ps, psb, nc = g.S, g.sb, g.IN, g.DR, g.C, g.ps, g.psb, g.nc
    S.barrier()
    sb.off = g.persist_mark
    ident = C['ident']
    w1 = [sb.alloc(f"nw1{t}", [64, 32, 128], BF16) for t in range(2)]
    w2 = [sb.alloc(f"nw2{t}", [128, 64], BF16) for t in range(2)]
    posT = [sb.alloc(f"npos{t}", [64, 32], BF16) for t in range(2)]
    KC = [sb.alloc(f"nKC{gg}", [64, 256], BF16) for gg in range(2)]
    VCc = [sb.alloc(f"nVC{gg}", [128, 2, 64], BF16) for gg in range(2)]
    bias = sb.alloc("nbias", [128, 2], F32)
    mark = sb.off
    stg = sb.alloc("nstg", [64, 32, 128], F32)
    stg2 = sb.alloc("nstg2", [128, 128], F32)
    src = [[sb.alloc(f"nsrc{t}{gg}", [64, S_LEN], BF16) for gg in range(2)] for t in range(2)]
    hs = [sb.alloc(f"nhs{i}", [128, 256], BF16) for i in range(2)]
    for i_ in range(2):
        S.op('pool', lambda e, i_=i_: e.memset(hs[i_][:, :], 0.0), writes=[f'nhs{i_}'])
    for t, (n1, n2, npos) in enumerate((('ck_w1', 'ck_w2', 'pos_kT'), ('cv_w1', 'cv_w2', 'pos_vT'))):
        S.dma(stg[:, :, :], IN[n1][l].rearrange("(a d) h -> d a h", d=64), writes=['nstg'])
        S.op('dve', lambda e, t=t: e.tensor_copy(out=w1[t][:, :, :], in_=stg[:, :, :]), reads=['nstg'], writes=[f'nw1{t}'])
        S.dma(stg2[:, 0:64], IN[n2][l], writes=['nstg2'])
        S.dma(stg2[0:64, 64:96], IN[npos][l], writes=['nstg2'])
        S.op('dve', lambda e, t=t: e.tensor_copy(out=w2[t][:, :], in_=stg2[:, 0:64]), reads=['nstg2'], writes=[f'nw2{t}'])
        S.op('dve', lambda e, t=t: e.tensor_copy(out=posT[t][:, :], in_=stg2[0:64, 64:96]), reads=['nstg2'], writes=[f'npos{t}'])
        for gg in range(2):
            if t == 0:
                S.dma(src[t][gg][:, :], DR['R64T'][(6 + gg) * 64:(7 + gg) * 64, :], writes=[f'nsrc{t}{gg}'])
            else:
                S.dma(src[t][gg][:, :], DR['VCT'][gg * 64:(gg + 1) * 64, :], writes=[f'nsrc{t}{gg}'])
    for t in range(2):
        for a in range(32):
            S.op('pe', lambda e, t=t, a=a: e.matmul(ps[4][:, t:t + 1], lhsT=w1[t][:, a, :], rhs=posT[t][:, a:a + 1], start=(a == 0), stop=(a == 31)),
                 reads=[f'nw1{t}', f'npos{t}'], writes=['ps4'])
    S.op('dve', lambda e: e.tensor_copy(out=bias[:, :], in_=ps[4][:, 0:2]), reads=['ps4'], writes=['nbias'])
    for t in range(2):
        for gg in range(2):
            bk = t * 2 + gg
            hb = hs[bk % 2]
            for a in range(32):
                rhs = fap(src[t][gg][:, a:a + 1], [(16, 255)])
                S.op('pe', lambda e, t=t, a=a, bk=bk, rhs=rhs: e.matmul(ps[bk][:, 0:255], lhsT=w1[t][:, a, :], rhs=rhs, start=(a == 0), stop=(a == 31)),
                     reads=[f'nw1{t}', f'nsrc{t}{gg}'], writes=[f'ps{bk}'])
            S.op('act', lambda e, t=t, bk=bk, hb=hb: e.activation(out=hb[:, 0:255], in_=ps[bk][:, 0:255], func=AF.Silu, bias=bias[:, t:t + 1]),
                 reads=[f'ps{bk}', 'nbias'], writes=[f'nhs{bk % 2}'])
            if t == 0:
                S.op('pe', lambda e, t=t, hb=hb: e.matmul(ps[5][0:64, 0:255], lhsT=w2[t][:, :], rhs=hb[:, 0:255], start=True, stop=True),
                     reads=[f'nhs{bk % 2}', f'nw2{t}'], writes=['ps5'])
                S.op('dve', lambda e, gg=gg: e.tensor_copy(out=KC[gg][:, 0:255], in_=ps[5][0:64, 0:255]), reads=['ps5'], writes=[f'nKC{gg}'])
            else:
                S.op('pe', lambda e, t=t, hb=hb: e.matmul(ps[5][:, 256:320], lhsT=hb[:, 0:128], rhs=w2[t][:, :], start=True, stop=True),
                     reads=[f'nhs{bk % 2}', f'nw2{t}'], writes=['ps5'])
                S.op('pe', lambda e, t=t, hb=hb: e.matmul(ps[5][:, 320:384], lhsT=hb[:, 128:256], rhs=w2[t][:, :], start=True, stop=True, skip_group_check=True),
                     reads=[f'nhs{bk % 2}', f'nw2{t}'], writes=['ps5'])
                S.op('dve', lambda e, gg=gg: e.tensor_copy(out=VCc[gg][:, :, :], in_=ps[5][:, 256:384].rearrange("p (a d) -> p a d", d=64)), reads=['ps5'], writes=[f'nVC{gg}'])
    S.barrier()
    sb.off = mark
    KSa = sb.alloc("nKSa", [128, S_LEN], BF16)
    KWT = sb.alloc("nKWT", [128, S_LEN], BF16)
    VSx = sb.alloc("nVSx", [128, NT, 65], BF16)
    VWx = sb.alloc("nVWx", [128, NT, 65], BF16)
    ohst = sb.alloc("nohst", [128, S_LEN], F32)
    Qa = [[sb.alloc(f"nQa{p}{hh}", [128, 512], BF16) for hh in range(3)] for p in range(2)]
    g.pbf = [sb.alloc(f"pbf{i}", [128, 512], BF16) for i in range(3)]
    gt = [sb.alloc(f"ngt{p}", [128, 4, 18], F32) for p in range(2)]
    zt = [sb.alloc(f"nzt{p}", [128, 4, 192], F32) for p in range(2)]
    nacc = [sb.alloc(f"nacc{p}", [128, 4, 3, 64], F32) for p in range(2)]
    mixt = [sb.alloc(f"nmix{p}", [128, 4, 192], BF16) for p in range(2)]
    ec = [sb.alloc(f"nec{i}", [128, 256], F32) for i in range(2)]
    pm = [sb.alloc(f"npm{i}", [128, 256], F32) for i in range(2)]
    pmb = [sb.alloc(f"npmb{i}", [128, 256], BF16) for i in range(2)]
    pT = [sb.alloc(f"npT{i}", [128, 2, 128], BF16) for i in range(2)]
    P3 = sb.alloc("nP3", [128, 260], F32)
    imp = sb.alloc("nimp", [128, 64], F32)
    imp2 = sb.alloc("nimp2", [128, 64], F32)
    rep = [sb.alloc(f"nrep{i}", [128, 64], F32) for i in range(2)]
    m8 = sb.alloc("nm8", [128, 16], F32)
    selb = sb.alloc("nselb", [128, 128], BF16)
    st = sb.alloc("nst", [128, 64], F32)
    rec = sb.alloc("nrec", [128, 8], F32)
    S.dma(ohst[64:128, :], IN['onehot'][64:128, :], writes=['nohst'])
    S.op('pool', lambda e: e.tensor_copy(out=KSa[64:128, :], in_=ohst[64:128, :]), reads=['nohst'], writes=['nKSa.oh'])
    S.op('dve', lambda e: e.memset(selb[:, :], 0.0), writes=['nselb'])
    S.op('dve', lambda e: e.memset(KWT[64:128, :], 0.0), writes=['nKWT.z'])
    for p_ in range(2):
        for hh_ in range(3):
            S.op('pool', lambda e, p_=p_, hh_=hh_: e.memset(Qa[p_][hh_][64:128, :], 0.0), writes=[f'nQa{p_}{hh_}.s{j_}' for j_ in range(4)])
    for i_ in range(2):
        S.op('pool', lambda e, i_=i_: e.memset(pmb[i_][:, :], 0.0), writes=[f'npmb{i_}'])
    stc = [0]

    def cmp_select(gg, c, p):
        cs = slice(c * 512, (c + 1) * 512)
        S.dma(gt[p][:, :, :], DR['GATE'][cs, :].rearrange("(j p) c -> p j c", p=128), writes=[f'ngt{p}'])
        S.dma(zt[p][:, :, :], DR['Z'][cs, gg * 192:(gg + 1) * 192].rearrange("(j p) d -> p j d", p=128), writes=[f'nzt{p}'])
        for hh in range(3):
            S.dma(Qa[p][hh][0:64, :], DR['R64T'][(gg * 3 + hh) * 64:(gg * 3 + hh + 1) * 64, cs], writes=[f'nQa{p}{hh}.q'])
        for _ in range(3):
            yield
        items = [(j, hh) for j in range(4) for hh in range(3)]
        NI = len(items)

        def geo(n):
            j, hh = items[n]
            i = 4 * c + j
            ncmp = min(255, 8 * i + 8)
            tiles = [(0, min(128, ncmp))] + ([(128, ncmp - 128)] if ncmp > 128 else [])
            return j, hh, i, ncmp, tiles, n % 2, (n % 8) * 4

        def s1(n):
            j, hh, i, ncmp, tiles, k, s0 = geo(n)
            gcol = (gg * 3 + hh) * 3
            if hh == 0:
                S.op('pool', lambda e: e.memset(P3[:, :], 0.0), writes=['nP3'])
            S.op('pe', lambda e: e.matmul(ps[5][:, 0:ncmp], lhsT=Qa[p][hh][0:64, j * 128:(j + 1) * 128], rhs=KC[gg][:, 0:ncmp], start=True, stop=True),
                 reads=[f'nQa{p}{hh}.q', f'nKC{gg}'], writes=['ps5'])
            S.op('act', lambda e: e.activation(out=ec[k][:, 0:ncmp], in_=ps[5][:, 0:ncmp], func=AF.Exp, scale=0.125), reads=['ps5'], writes=[f'nec{k}'])
            S.op('dve', lambda e: e.memset(st[:, s0:s0 + 1], 0.0), writes=[f'nst{s0}'])
            S.op('dve', lambda e: e.scalar_tensor_tensor(out=pm[k][:, 0:ncmp], in0=ec[k][:, 0:ncmp], scalar=1.0, in1=C['mbig'][:, 248 - 8 * i:248 - 8 * i + ncmp],
                                                         op0=ALU.mult, op1=ALU.mult, accum_out=st[:, s0:s0 + 1]),
                 reads=[f'nec{k}', 'c_mbig', f'nst{s0}'], writes=[f'npm{k}', f'nst{s0}'])
            S.op('pool', lambda e: e.tensor_copy(out=pmb[k][:, 0:ncmp], in_=pm[k][:, 0:ncmp]), reads=[f'npm{k}'], writes=[f'npmb{k}'])
            S.op('dve', lambda e: e.tensor_scalar(out=st[:, s0 + 1:s0 + 2], in0=st[:, s0:s0 + 1], scalar1=1e-30, scalar2=None, op0=ALU.max), reads=[f'nst{s0}'], writes=[f'nst{s0}'])
            S.op('dve', lambda e: e.reciprocal(out=st[:, s0 + 1:s0 + 2], in_=st[:, s0 + 1:s0 + 2]), reads=[f'nst{s0}'], writes=[f'nst{s0}'])
            S.op('dve', lambda e: e.scalar_tensor_tensor(out=P3[:, 1:1 + ncmp], in0=pm[k][:, 0:ncmp], scalar=st[:, s0 + 1:s0 + 2], in1=P3[:, 1:1 + ncmp],
                                                         op0=ALU.mult, op1=ALU.add), reads=[f'npm{k}', f'nst{s0}', 'nP3'], writes=['nP3'])
            S.op('dve', lambda e: e.tensor_tensor(out=st[:, s0 + 2:s0 + 3], in0=st[:, s0 + 1:s0 + 2], in1=gt[p][:, j, gcol:gcol + 1], op=ALU.mult),
                 reads=[f'nst{s0}', f'ngt{p}'], writes=[f'nst{s0}'])
            if hh == 2:
                S.op('dve', lambda e: e.tensor_reduce(out=imp[:, :], in_=P3[:, 0:256].rearrange("p (b m) -> p b m", m=4), axis=AX.X, op=ALU.add), reads=['nP3'], writes=['nimp'])
                S.op('dve', lambda e: e.tensor_tensor(out=imp[:, :], in0=imp[:, :], in1=fap(P3[:, 4:5], [(4, 64)]), op=ALU.add), reads=['nP3', 'nimp'], writes=['nimp'])
                S.op('dve', lambda e: e.tensor_tensor(out=imp2[:, :], in0=imp[:, :], in1=C['selkeep'][:, 62 - 2 * i:126 - 2 * i], op=ALU.mult), reads=['nimp', 'c_selkeep'], writes=['nimp2'])
                S.op('dve', lambda e: e.tensor_tensor(out=imp2[:, :], in0=imp2[:, :], in1=C['seladd'][:, 62 - 2 * i:126 - 2 * i], op=ALU.add), reads=['nimp2', 'c_seladd'], writes=['nimp2'])
                S.op('dve', lambda e: e.memset(imp2[:, 0:1], 1.2e9), reads=['nimp2'], writes=['nimp2'])
                S.op('dve', lambda e: e.max(out=m8[:, 0:8], in_=imp2[:, :]), reads=['nimp2'], writes=['nm8'])
                S.op('dve', lambda e: e.match_replace(out=rep[0][:, :], in_to_replace=m8[:, 0:8], in_values=imp2[:, :], imm_value=-3e9), reads=['nimp2', 'nm8'], writes=['nrep0'])
                S.op('dve', lambda e: e.max(out=m8[:, 8:16], in_=rep[0][:, :]), reads=['nrep0'], writes=['nm8'])
                S.op('dve', lambda e: e.match_replace(out=rep[1][:, :], in_to_replace=m8[:, 8:16], in_values=rep[0][:, :], imm_value=-3e9), reads=['nrep0', 'nm8'], writes=['nrep1'])
                S.op('dve', lambda e: e.tensor_scalar(out=selb[:, 64:128], in0=rep[1][:, :], scalar1=-2e9, scalar2=-30000.0, op0=ALU.is_gt, op1=ALU.mult), reads=['nrep1'], writes=['nselb'])

        def s2(n):
            j, hh, i, ncmp, tiles, k, s0 = geo(n)
            for ti, (c0, w) in enumerate(tiles):
                S.op('pe', lambda e, ti=ti, c0=c0: e.transpose(psb[6][:, ti * 128:(ti + 1) * 128], pmb[k][:, c0:c0 + 128], ident[:, :]),
                     reads=[f'npmb{k}', 'c_ident'], writes=['ps6'])
            nt_ = len(tiles)
            S.op('act', lambda e: e.copy(out=pT[k][:, 0:nt_, :], in_=psb[6][:, 0:128 * nt_].rearrange("p (a q) -> p a q", q=128)), reads=['ps6'], writes=[f'npT{k}'])

        def s3(n):
            j, hh, i, ncmp, tiles, k, s0 = geo(n)
            for ti, (c0, w) in enumerate(tiles):
                S.op('pe', lambda e, ti=ti, w=w: e.matmul(ps[7][:, hh * 64:(hh + 1) * 64], lhsT=pT[k][0:w, ti, :], rhs=VCc[gg][0:w, ti, :],
                                                        start=(ti == 0), stop=(ti == len(tiles) - 1)),
                     reads=[f'npT{k}', f'nVC{gg}'], writes=['ps7'])
            S.op('dve', lambda e: e.tensor_scalar(out=nacc[p][:, j, hh, :], in0=ps[7][:, hh * 64:(hh + 1) * 64], scalar1=st[:, s0 + 2:s0 + 3], scalar2=None, op0=ALU.mult),
                 reads=['ps7', f'nst{s0}'], writes=[f'nacc{p}.{hh}'])

        def sel_b(j):
            S.op('pe', lambda e: e.transpose(psb[6][:, 256:384], selb[:, :], ident[:, :]), reads=['nselb', 'c_ident'], writes=['ps6'])

        def sel_c(j):
            for hh in range(3):
                if hh == 1:
                    S.op('dve', lambda e, hh=hh: e.tensor_copy(out=Qa[p][hh][64:128, j * 128:(j + 1) * 128], in_=psb[6][64:128, 256:384]), reads=['ps6'], writes=[f'nQa{p}{hh}.s{j}'])
                else:
                    S.op('act', lambda e, hh=hh: e.copy(out=Qa[p][hh][64:128, j * 128:(j + 1) * 128], in_=psb[6][64:128, 256:384]), reads=['ps6'], writes=[f'nQa{p}{hh}.s{j}'])

        for m in range(NI + 3):
            if m < NI:
                s1(m)
            if 0 <= m - 1 < NI:
                s2(m - 1)
            if 0 <= m - 2 < NI:
                s3(m - 2)
            if m >= 3 and (m - 3) % 3 == 0 and (m - 3) // 3 < 4:
                sel_b((m - 3) // 3)
                sel_c((m - 3) // 3)
            yield

    for gg in range(2):
        S.dma(KSa[0:64, :], DR['R64T'][(8 + gg) * 64:(9 + gg) * 64, :], writes=['nKSa.k'])
        S.dma(KWT[0:64, :], DR['R64T'][(10 + gg) * 64:(11 + gg) * 64, :], writes=['nKWT'])
        S.op('pool', lambda e: e.memset(VSx[:, :, :], 1.0), writes=['nVSx'])
        S.op('pool', lambda e: e.memset(VWx[:, :, :], 1.0), writes=['nVWx'])
        S.dma(VSx[:, :, 0:64], DR['VSW'][:, gg * 64:(gg + 1) * 64].rearrange("(t p) d -> p t d", p=128), writes=['nVSx'])
        S.dma(VWx[:, :, 0:64], DR['VSW'][:, 128 + gg * 64:128 + (gg + 1) * 64].rearrange("(t p) d -> p t d", p=128), writes=['nVWx'])
        jobs = []
        for c in range(8):
            p = c % 2
            for br in (1, 2):
                for hh in range(3):
                    qk = [f'nQa{p}{hh}.q'] + ([f'nQa{p}{hh}.s{j}' for j in range(4)] if br == 1 else [])
                    units = []
                    if br == 1:
                        for kt in range(4 * c + 4):
                            i = kt - 4 * c
                            units.append(dict(lhsT=KSa[:, kt * 128:(kt + 1) * 128], rhs=Qa[p][hh][:, :], rk=['nKSa.k', 'nKSa.oh'] + qk,
                                              mask=((i, C['mtri'][:, 0:128]) if i >= 0 else None), mk='c_mtri',
                                              subs=(list(range(i, 4)) if i >= 0 else [0, 1, 2, 3]), vext=VSx[:, kt, :], vk='nVSx'))
                    else:
                        for kt in range(max(0, 4 * c - 4), 4 * c + 4):
                            o = kt - 4 * c
                            subs = list(range(0, o + 5)) if o < 0 else list(range(o, 4))
                            units.append(dict(lhsT=KWT[:, kt * 128:(kt + 1) * 128], rhs=Qa[p][hh][:, :], rk=['nKWT', 'nKWT.z'] + qk + [f'nQa{p}{hh}.s{j_}' for j_ in range(4)],
                                              mask=((o + 4, C['mtri'][:, 256:384]) if o < 0 else (o, C['mtri'][:, 0:128])), mk='c_mtri', subs=subs, vext=VWx[:, kt, :], vk='nVWx'))

                    def epi(ab, gg=gg, c=c, p=p, br=br, hh=hh):
                        acc = ps[ab][:, 0:260].rearrange("p (j e) -> p j e", e=65)
                        gcol = (gg * 3 + hh) * 3 + br
                        S.op('dve', lambda e: e.reciprocal(out=rec[:, 0:4], in_=acc[:, :, 64]), reads=[f'ps{ab}'], writes=['nrec'])
                        S.op('dve', lambda e: e.tensor_tensor(out=rec[:, 4:8], in0=rec[:, 0:4], in1=gt[p][:, :, gcol], op=ALU.mult), reads=['nrec', f'ngt{p}'], writes=['nrec'])
                        for j in range(4):
                            S.op('dve', lambda e, j=j: e.scalar_tensor_tensor(out=nacc[p][:, j, hh, :], in0=acc[:, j, 0:64], scalar=rec[:, 4 + j:5 + j], in1=nacc[p][:, j, hh, :],
                                                                              op0=ALU.mult, op1=ALU.add), reads=[f'ps{ab}', 'nrec', f'nacc{p}.{hh}'], writes=[f'nacc{p}.{hh}'])
                        if br == 2:
                            S.op('pool', lambda e: e.tensor_tensor(out=mixt[p][:, :, hh * 64:(hh + 1) * 64], in0=nacc[p][:, :, hh, :], in1=zt[p][:, :, hh * 64:(hh + 1) * 64], op=ALU.mult),
                                 reads=[f'nacc{p}.{hh}', f'nzt{p}'], writes=[f'nmix{p}.{hh}'])
                            if hh == 2:
                                S.dma(DR['MIX'][c * 512:(c + 1) * 512, gg * 192:(gg + 1) * 192].rearrange("(j p) d -> p j d", p=128), mixt[p][:, :, :],
                                      reads=[f'nmix{p}.{h2}' for h2 in range(3)])
                    job = dict(units=units, epi=epi)
                    if br == 1 and hh == 0:
                        if c == 0:
                            job['pre'] = (lambda gg=gg: [None for _ in cmp_select(gg, 0, 0)])
                        else:
                            job['need_bg'] = True
                        if c + 1 < 8:
                            job['bg'] = (lambda gg=gg, c=c: cmp_select(gg, c + 1, (c + 1) % 2))
                    jobs.append(job)
        attn_stream(g, jobs, 0.125)


def phase_outproj(g, l, last):
    S, sb, IN, DR, C, ps, psb = g.S, g.sb, g.IN, g.DR, g.C, g.ps, g.psb
    S.barrier()
    sb.off = g.persist_mark
    WO = sb.alloc("WO", [128, 8, D], BF16)
    wst = [sb.alloc(f"wst{i}", [128, D], F32) for i in range(2)]
    mx = [sb.alloc(f"omx{i}", [128, D], BF16) for i in range(2)]
    mT = [sb.alloc(f"omT{i}", [128, 8, 128], BF16) for i in range(2)]
    xt = [sb.alloc(f"oxt{i}", [128, D], F32) for i in range(2)]
    yt = [sb.alloc(f"oyt{i}", [128, D], F32) for i in range(2)]
    junk = sb.alloc("ojunk", [128, D], BF16)
    ss = sb.alloc("oss", [128, NT], F32)
    rs = sb.alloc("ors", [128, NT], F32)
    ident = C['ident']
    for k in range(8):
        b = k % 2
        S.dma(wst[b][:, :], IN['w_out'][l, k * 128:(k + 1) * 128, :], writes=[f'wst{b}'])
        S.op(['dve', 'pool'][k % 2], lambda e, b=b, k=k: e.tensor_copy(out=WO[:, k, :], in_=wst[b][:, :]), reads=[f'wst{b}'], writes=[f'WO{k}'])
    S.op('dve', lambda e: e.memset(ss[:, :], 0.0), writes=['oss'])
    xsrc = IN['x'] if l == 0 else DR['X1']
    mhalf = sb.alloc("omhalf", [128, 2], F32)
    S.op('pool', lambda e: e.memset(mhalf[:, :], -0.5), writes=['omhalf'])

    def o1(tt):
        b = tt % 2
        tts = slice(tt * 128, (tt + 1) * 128)
        S.dma(mx[b][:, :], DR['MIX'][tts, :], writes=[f'omx{b}'], eng='pool')
        S.dma(xt[b][:, :], xsrc[tts, :], writes=[f'oxt{b}'], eng='pool')
        for c in range(8):
            S.op('pe', lambda e, c=c: e.transpose(psb[7 - b][:, c * 128:(c + 1) * 128], mx[b][:, c * 128:(c + 1) * 128], ident[:, :]),
                 reads=[f'omx{b}', 'c_ident'], writes=[f'ps{7 - b}'])
        S.op('act', lambda e: e.copy(out=mT[b][:, :, :], in_=psb[7 - b][:, :].rearrange("p (c t) -> p c t", t=128)), reads=[f'ps{7 - b}'], writes=[f'omT{b}'])

    def o2(tt):
        b = tt % 2
        tts = slice(tt * 128, (tt + 1) * 128)
        for hf in range(2):
            bk = 2 * b + hf
            for k in range(8):
                S.op('pe', lambda e, k=k, hf=hf, bk=bk: e.matmul(ps[bk][:, :], lhsT=mT[b][:, k, :], rhs=WO[:, k, hf * 512:(hf + 1) * 512], start=(k == 0), stop=(k == 7)),
                     reads=[f'omT{b}', f'WO{k}'], writes=[f'ps{bk}'])
            S.op('dve', lambda e, hf=hf, bk=bk: e.tensor_tensor(out=yt[b][:, hf * 512:(hf + 1) * 512], in0=ps[bk][:, :], in1=g.gate_b[:, hf * 512:(hf + 1) * 512], op=ALU.mult),
                 reads=[f'ps{bk}', 'gate_b'], writes=[f'oyt{b}.{hf}'])
        S.op('pool', lambda e: e.tensor_tensor(out=yt[b][:, :], in0=yt[b][:, :], in1=xt[b][:, :], op=ALU.add),
             reads=[f'oyt{b}.0', f'oyt{b}.1', f'oxt{b}'], writes=[f'oyt{b}.0', f'oyt{b}.1'])
        if not last:
            S.dma(DR['X1'][tts, :], yt[b][:, :], reads=[f'oyt{b}.0', f'oyt{b}.1'])
        else:
            S.op('act', lambda e: e.activation(out=junk[:, :], in_=yt[b][:, :], func=AF.Square, accum_out=ss[:, tt:tt + 1]),
                 reads=[f'oyt{b}.0', f'oyt{b}.1', 'oss'], writes=['ojunk', f'oss{tt}'])
            S.op('dve', lambda e: e.tensor_scalar(out=rs[:, tt:tt + 1], in0=ss[:, tt:tt + 1], scalar1=1.0 / D, scalar2=EPS, op0=ALU.mult, op1=ALU.add),
                 reads=[f'oss{tt}'], writes=[f'ors{tt}'])
            S.op('pool', lambda e: e.tensor_tensor(out=rs[:, tt:tt + 1], in0=rs[:, tt:tt + 1], in1=mhalf[:, 0:1], op=ALU.pow), reads=[f'ors{tt}', 'omhalf'], writes=[f'ors{tt}'])
            S.op('dve', lambda e: e.scalar_tensor_tensor(out=yt[b][:, :], in0=yt[b][:, :], scalar=rs[:, tt:tt + 1], in1=g.fnorm_b[:, :], op0=ALU.mult, op1=ALU.mult),
                 reads=[f'oyt{b}.0', f'oyt{b}.1', f'ors{tt}', 'fnorm_b'], writes=[f'oyt{b}.0', f'oyt{b}.1'])
            S.dma(g.out[tts, :], yt[b][:, :], reads=[f'oyt{b}.0', f'oyt{b}.1'], is_output=True)

    o1(0)
    for it in range(NT):
        if it + 1 < NT:
            o1(it + 1)
        o2(it)


class K:
    pass


def build(dbg=(), phases=('inproj', 'mla', 'sb', 'nsa', 'outproj'), nlayers=DEPTH):
    nc = bass.Bass("TRN2", target_bir_lowering=False)
    S = Sched(nc)
    sb = SB(nc)
    g = K()
    g.nc, g.S, g.sb = nc, S, sb
    IN = {}
    for n, (shp, dt) in IN_SHAPES.items():
        IN[n] = nc.dram_tensor(n, shp, dt, kind="ExternalInput").ap()
    for n, shp in CONST_SHAPES.items():
        IN[n] = nc.dram_tensor(n, shp, F32, kind="ExternalInput").ap()
    g.IN = IN
    out = nc.dram_tensor("out", [S_LEN, D], F32, kind="ExternalOutput").ap()
    g.out = out

    def scratch(name, shape, dt):
        kind = "ExternalOutput" if name in dbg else "Internal"
        return nc.dram_tensor(name, shape, dt, kind=kind).ap()
    DR = {}
    for name, shape, dt in [
        ('R64T', [768, S_LEN], BF16), ('VCT', [128, S_LEN], BF16), ('SQKT', [640, S_LEN], BF16),
        ('MKPE', [32, S_LEN], BF16), ('VSW', [S_LEN, 256], BF16), ('SV', [S_LEN, 320], BF16),
        ('Z', [S_LEN, 1024], F32), ('GATE', [S_LEN, 18], F32), ('MV', [S_LEN, 320], BF16),
        ('MQN', [320, S_LEN], BF16), ('MQR', [160, S_LEN], BF16), ('MKN', [320, S_LEN], BF16),
        ('MIX', [S_LEN, 1024], BF16), ('X1', [S_LEN, D], F32), ('GROW', [1, D], F32),
    ]:
        DR[name] = scratch(name, shape, dt)
    g.DR = DR
    g.dbg = dbg

    def dbgout(name, ap, keys, dt=None):
        if name not in dbg:
            return
        t = nc.dram_tensor(name, list(ap.shape), dt or ap.dtype, kind='ExternalOutput').ap()
        S.dma(t, ap, reads=keys, is_output=True)
    g.dbgout = dbgout

    g.ps = [nc.alloc_psum_tensor(f"bank{i}", [128, 512], F32) for i in range(8)]
    g.psb = [t.bitcast(BF16) for t in g.ps]

    C = {}
    stage = nc.alloc_sbuf_tensor_at("cstage", [128, 4096], F32, offset=200 * 1024)

    def load_const(name, dt, shape2d):
        t = sb.alloc("c_" + name, [128, shape2d], dt)
        src = IN[name]
        if len(src.shape) == 3:
            src = src.rearrange("p a b -> p (a b)")
        if dt == F32:
            S.dma(t[:, :], src, writes=['c_' + name])
        else:
            S.dma(stage[:, 0:shape2d], src, writes=['cstage'])
            S.op('dve', lambda e: e.tensor_copy(out=t[:, :], in_=stage[:, 0:shape2d]), reads=['cstage'], writes=['c_' + name])
        C[name] = t
    load_const('ident', BF16, 128)
    load_const('m_causal', BF16, 2048)
    load_const('m_strict', BF16, 2048)
    load_const('m_win', BF16, 4096)
    load_const('tri', BF16, 128)
    load_const('mtri', BF16, 384)
    load_const('inv', F32, 48)
    load_const('mbig', F32, 503)
    load_const('selkeep', F32, 126)
    load_const('seladd', F32, 126)
    g.C = C
    ones_bf = sb.alloc("ones_bf", [128, 128], BF16)
    S.op('dve', lambda e: e.memset(ones_bf[:, :], 1.0), writes=['ones_bf'])
    ones_f = sb.alloc("ones_f", [128, 128], F32)
    S.op('dve', lambda e: e.memset(ones_f[:, :], 1.0), writes=['ones_f'])
    g.ones_bf, g.ones_f = ones_bf, ones_f

    pos_i = sb.alloc("pos_i", [128, NT], I32)
    pos_f = sb.alloc("pos_f", [128, NT], F32)
    S.dma(pos_i[:, :], IN['pos_pt'], writes=['pos_i'])
    S.op('dve', lambda e: e.tensor_copy(out=pos_f[:, :], in_=pos_i[:, :]), reads=['pos_i'], writes=['pos_f'])
    tabs = {}
    for nm, i0, n in (('64', 0, 32), ('32', 32, 16)):
        ang = nc.alloc_sbuf_tensor_at("ang" + nm, [128, NT, n], F32, offset=(176 if nm == "64" else 192) * 1024)
        tmp = nc.alloc_sbuf_tensor_at("angt" + nm, [128, NT, n], F32, offset=(182 if nm == "64" else 195) * 1024)
        cs = sb.alloc("cos" + nm, [128, NT, n], F32)
        sn = sb.alloc("sin" + nm, [128, NT, n], F32)
        a_pos = fap(pos_f[:, :], [(1, NT), (0, n)])
        a_inv = fap(C['inv'][:, i0:i0 + n], [(0, NT), (1, n)])
        S.op('dve', lambda e, ang=ang, a_pos=a_pos, a_inv=a_inv: e.tensor_tensor(out=ang[:, :, :], in0=a_pos, in1=a_inv, op=ALU.mult),
             reads=['pos_f', 'c_inv'], writes=['ang' + nm])
        ti = nc.alloc_sbuf_tensor_at("angi" + nm, [128, NT, n], I32, offset=(188 if nm == "64" else 198) * 1024)
        for which, shift, dst in (('s', 0.0, sn), ('c', 0.5 * PI, cs)):
            S.op('dve', lambda e, ang=ang, tmp=tmp, shift=shift: e.tensor_scalar(out=tmp[:, :, :], in0=ang[:, :, :], scalar1=shift, scalar2=1.0 / (2 * PI), op0=ALU.add, op1=ALU.mult),
                 reads=['ang' + nm, 'sin' + nm, 'cos' + nm], writes=['angt' + nm])
            S.op('dve', lambda e, tmp=tmp, ti=ti: e.tensor_copy(out=ti[:, :, :], in_=tmp[:, :, :]), reads=['angt' + nm], writes=['angi' + nm])
            S.op('dve', lambda e, tmp=tmp, ti=ti: e.tensor_copy(out=tmp[:, :, :], in_=ti[:, :, :]), reads=['angi' + nm], writes=['angt' + nm])
            S.op('dve', lambda e, ang=ang, tmp=tmp: e.scalar_tensor_tensor(out=tmp[:, :, :], in0=tmp[:, :, :], scalar=-2 * PI, in1=ang[:, :, :], op0=ALU.mult, op1=ALU.add),
                 reads=['angt' + nm, 'ang' + nm], writes=['angt' + nm])
            S.op('dve', lambda e, tmp=tmp, shift=shift: e.tensor_scalar(out=tmp[:, :, :], in0=tmp[:, :, :], scalar1=shift, scalar2=None, op0=ALU.add),
                 reads=['angt' + nm], writes=['angt' + nm])
            S.op('dve', lambda e, tmp=tmp, ti=ti: e.tensor_scalar(out=ti[:, :, :].bitcast(F32), in0=tmp[:, :, :], scalar1=PI, scalar2=2 * PI, op0=ALU.is_gt, op1=ALU.mult),
                 reads=['angt' + nm], writes=['angi' + nm])
            S.op('dve', lambda e, tmp=tmp, ti=ti: e.tensor_tensor(out=tmp[:, :, :], in0=tmp[:, :, :], in1=ti[:, :, :].bitcast(F32), op=ALU.subtract),
                 reads=['angt' + nm, 'angi' + nm], writes=['angt' + nm])
            S.op('act', lambda e, tmp=tmp, dst=dst: e.activation(out=dst[:, :, :], in_=tmp[:, :, :], func=AF.Sin),
                 reads=['angt' + nm], writes=[('sin' if which == 's' else 'cos') + nm])
        tabs[nm] = (cs, sn)
    g.tabs = tabs
    g.gate_b = sb.alloc("gate_b", [128, D], F32)
    g.modA = sb.alloc("modA", [128, 8], F32)
    g.modS = sb.alloc("modS", [128, 8], F32)
    g.fnorm_b = sb.alloc("fnorm_b", [128, D], F32)
    S.dma(g.fnorm_b[:, :], bass.AP(tensor=IN['final_norm'].tensor, offset=0, ap=[[0, 128], [1, D]]), writes=['fnorm_b'])
    g.persist_mark = sb.off

    for l in range(nlayers):
        phase_mod(g, l)
        if 'inproj' in phases:
            phase_inproj(g, l)
        if 'mla' in phases:
            phase_mla(g, l)
        if 'sb' in phases:
            phase_sb(g, l)
        if 'nsa' in phases:
            phase_nsa(g, l)
        if 'outproj' in phases:
            phase_outproj(g, l, last=(l == nlayers - 1))
    S.barrier()
    S.emit()
    return nc


_NC_CACHE = {}


def kernel(**inputs):
    inp = {k: np.asarray(v) for k, v in inputs.items()}
    maps = prep_inputs(inp)
    if 'nc' not in _NC_CACHE:
        _NC_CACHE['nc'] = build()
    nc = _NC_CACHE['nc']
    res = run_bass_kernel_spmd(nc, maps, core_ids=list(range(8)))
    out = np.stack([np.asarray(r['out'], dtype=np.float32) for r in res.results], axis=0)
    return out.reshape(8, S_LEN, D)
```
